# Optimizing a Trainium2 kernel written in Bass

```python
import math
import jax, jax.numpy as jnp
from jax import lax
import numpy as np

D_MODEL = 1024
BATCH = 8
SEQ = 2048
DEPTH = 2

HEAD_DIM = 64
SWA_Q_HEADS = 8
SWA_KV_HEADS = 2
SWA_WINDOW = 128
SWA_BLOCK = 128
SWA_Q = SWA_Q_HEADS * HEAD_DIM
SWA_KV = SWA_KV_HEADS * HEAD_DIM
REL_BUCKETS = 32
REL_MAX_DIST = 128
SSM_CH = 256
SSM_GROUP = 16
SSM_GROUPS = SSM_CH // SSM_GROUP
SSM_STATE = 64
DT_MIN = 1e-3
DT_MAX = 1e-1
MLA_HEADS = 4
MLA_Q_RANK = 256
MLA_KV_RANK = 128
MLA_NOPE = 64
MLA_ROPE = 32
MLA_V = 64
MLA_BLOCK = 128
ROPE_THETA = 10000.0
D_FF = ((8 * D_MODEL // 3 + 255) // 256) * 256
EPS = 1e-6
NEG = -1e30

MIX_WIDTH = SWA_Q + SSM_CH + MLA_HEADS * MLA_V
IN_SIZES = [SWA_Q, SWA_KV, SWA_KV, SSM_CH, MLA_Q_RANK, MLA_KV_RANK, MLA_ROPE]
IN_COLS = sum(IN_SIZES)
IN_SPLITS = [int(v) for v in np.cumsum(IN_SIZES)[:-1]]

kernel_name = "hymba_swa_s5_mla_hybrid"


def rms_norm(x, g):
    xf = x.astype(jnp.float32)
    y = xf * lax.rsqrt(jnp.mean(xf * xf, axis=-1, keepdims=True) + EPS)
    return (y * g.astype(jnp.float32)).astype(x.dtype)


def t5_bucket(dist):
    n = jnp.maximum(dist, 0)
    max_exact = REL_BUCKETS // 2
    large = max_exact + (jnp.log(jnp.maximum(n, 1).astype(jnp.float32) / max_exact)
                         / math.log(REL_MAX_DIST / max_exact)
                         * (REL_BUCKETS - max_exact)).astype(jnp.int32)
    large = jnp.minimum(large, REL_BUCKETS - 1)
    return jnp.where(n < max_exact, n, large)


def band_distance():
    qi = jnp.arange(SWA_BLOCK)[:, None]
    kj = jnp.arange(2 * SWA_BLOCK)[None, :]
    return qi + SWA_BLOCK - kj, kj


def band_bias(rel_bias):
    dist, _ = band_distance()
    b = rel_bias[t5_bucket(dist)]
    return jnp.transpose(b, (2, 0, 1)).astype(jnp.float32)


def apply_rope(x, positions):
    r = x.shape[-1]
    half = r // 2
    inv_freq = jnp.power(ROPE_THETA, -jnp.arange(half, dtype=jnp.float32) * 2.0 / r)
    ang = positions.astype(jnp.float32)[:, :, None, None] * inv_freq
    cos, sin = jnp.cos(ang), jnp.sin(ang)
    xf = x.astype(jnp.float32)
    x1, x2 = xf[..., :half], xf[..., half:]
    return jnp.concatenate([x1 * cos - x2 * sin, x1 * sin + x2 * cos], -1).astype(x.dtype)


def swa_attention(q, k, v, sinks, bias):
    b_, s_ = q.shape[:2]
    nb = s_ // SWA_BLOCK
    g = SWA_Q_HEADS // SWA_KV_HEADS
    qb = q.reshape(b_, nb, SWA_BLOCK, SWA_KV_HEADS, g, HEAD_DIM)

    def band(t):
        tb = t.reshape(b_, nb, SWA_BLOCK, SWA_KV_HEADS, HEAD_DIM)
        prev = jnp.pad(tb, ((0, 0), (1, 0), (0, 0), (0, 0), (0, 0)))[:, :-1]
        return jnp.concatenate([prev, tb], axis=2)

    kb, vb = band(k), band(v)
    s = jnp.einsum('bnqhgd,bnkhd->bnhgqk', qb, kb).astype(jnp.float32) * (HEAD_DIM ** -0.5)
    s = s + bias.reshape(SWA_KV_HEADS, g, SWA_BLOCK, 2 * SWA_BLOCK)
    dist, kj = band_distance()
    blk_start = jnp.arange(nb)[:, None, None] * SWA_BLOCK
    valid = (dist >= 0) & (dist < SWA_WINDOW) & (blk_start + kj - SWA_BLOCK >= 0)
    s = jnp.where(valid[None, :, None, None], s, NEG)
    sink_col = jnp.broadcast_to(sinks.astype(jnp.float32).reshape(SWA_KV_HEADS, g, 1, 1),
                                s.shape[:-1] + (1,))
    p = jax.nn.softmax(jnp.concatenate([s, sink_col], axis=-1), axis=-1)[..., :-1]
    o = jnp.einsum('bnhgqk,bnkhd->bnqhgd', p.astype(v.dtype), vb)
    return o.reshape(b_, s_, SWA_Q)


def s5_mixer(u, a_re, a_im, log_dt, b_re, b_im, c_re, c_im, d, w_glu):
    b_, s_ = u.shape[:2]
    f32 = jnp.float32
    lam = lax.complex(a_re.astype(f32), a_im.astype(f32))
    dt = jnp.exp(log_dt.astype(f32))
    a_bar = jnp.exp(lam * dt[:, None])
    bmat = lax.complex(b_re.astype(f32), b_im.astype(f32))
    b_bar = ((a_bar - 1.0) / lam)[..., None] * bmat
    ug = u.astype(f32).reshape(b_, s_, SSM_GROUPS, SSM_GROUP)
    bu = jnp.einsum('bsgc,gpc->bsgp', ug.astype(jnp.complex64), b_bar)
    a_seq = jnp.broadcast_to(a_bar, bu.shape)

    def combine(e1, e2):
        a1, x1 = e1
        a2, x2 = e2
        return a1 * a2, a2 * x1 + x2

    _, states = lax.associative_scan(combine, (a_seq, bu), axis=1)
    cmat = lax.complex(c_re.astype(f32), c_im.astype(f32))
    y = jnp.real(jnp.einsum('bsgp,gcp->bsgc', states, cmat)) \
        + d.astype(f32).reshape(SSM_GROUPS, SSM_GROUP) * ug
    y = jax.nn.gelu(y.reshape(b_, s_, SSM_CH))
    y = y * jax.nn.sigmoid(y @ w_glu.astype(f32))
    return y.astype(u.dtype)


def mla_attention(c_q, c_kv, k_rope, positions, q_norm_g, w_q_up, kv_norm_g, w_kv_up):
    b_, s_ = c_q.shape[:2]
    q = (rms_norm(c_q, q_norm_g) @ w_q_up).reshape(b_, s_, MLA_HEADS, MLA_NOPE + MLA_ROPE)
    q_nope, q_pe = q[..., :MLA_NOPE], apply_rope(q[..., MLA_NOPE:], positions)
    kv = (rms_norm(c_kv, kv_norm_g) @ w_kv_up).reshape(b_, s_, MLA_HEADS, MLA_NOPE + MLA_V)
    k_nope, v = kv[..., :MLA_NOPE], kv[..., MLA_NOPE:]
    k_pe = apply_rope(k_rope[:, :, None, :], positions)[:, :, 0, :]
    scale = (MLA_NOPE + MLA_ROPE) ** -0.5
    nb = s_ // MLA_BLOCK
    key_pos = jnp.arange(s_)

    def attend(args):
        qn, qp, i = args
        s = (jnp.einsum('bqhd,bkhd->bhqk', qn, k_nope)
             + jnp.einsum('bqhr,bkr->bhqk', qp, k_pe)).astype(jnp.float32) * scale
        q_pos = i * MLA_BLOCK + jnp.arange(MLA_BLOCK)
        s = jnp.where(key_pos[None, :] <= q_pos[:, None], s, NEG)
        p = jax.nn.softmax(s, axis=-1).astype(v.dtype)
        return jnp.einsum('bhqk,bkhd->bqhd', p, v)

    qn_b = q_nope.reshape(b_, nb, MLA_BLOCK, MLA_HEADS, MLA_NOPE).swapaxes(0, 1)
    qp_b = q_pe.reshape(b_, nb, MLA_BLOCK, MLA_HEADS, MLA_ROPE).swapaxes(0, 1)
    o = lax.map(attend, (qn_b, qp_b, jnp.arange(nb)))
    return o.swapaxes(0, 1).reshape(b_, s_, MLA_HEADS * MLA_V)


def setup_inputs(seed: int = 0) -> dict:
    key = jax.random.key(seed)
    ks = list(jax.random.split(key, 32))
    nrm = lambda k, shape, s: jax.random.normal(k, shape, jnp.float32) * s
    gain = lambda k, shape: 1.0 + 0.02 * jax.random.normal(k, shape, jnp.float32)
    n_idx = jnp.arange(SSM_STATE, dtype=jnp.float32)
    return {
        "x": nrm(ks[0], (BATCH, SEQ, D_MODEL), 1.0),
        "positions": jnp.broadcast_to(jnp.arange(SEQ, dtype=jnp.int32)[None], (BATCH, SEQ)),
        "rel_bias": nrm(ks[1], (REL_BUCKETS, SWA_Q_HEADS), 0.5),
        "ln1_g": gain(ks[2], (DEPTH, D_MODEL)),
        "w_in": nrm(ks[3], (DEPTH, D_MODEL, IN_COLS), D_MODEL ** -0.5),
        "sinks": nrm(ks[4], (DEPTH, SWA_Q_HEADS), 0.5),
        "ssm_a_re": -0.5 * jnp.exp(nrm(ks[5], (DEPTH, SSM_GROUPS, SSM_STATE), 0.05)),
        "ssm_a_im": math.pi * n_idx + nrm(ks[6], (DEPTH, SSM_GROUPS, SSM_STATE), 0.01),
        "ssm_log_dt": jax.random.uniform(ks[7], (DEPTH, SSM_GROUPS), jnp.float32,
                                         math.log(DT_MIN), math.log(DT_MAX)),
        "ssm_b_re": nrm(ks[8], (DEPTH, SSM_GROUPS, SSM_STATE, SSM_GROUP), (2 * SSM_GROUP) ** -0.5),
        "ssm_b_im": nrm(ks[9], (DEPTH, SSM_GROUPS, SSM_STATE, SSM_GROUP), (2 * SSM_GROUP) ** -0.5),
        "ssm_c_re": nrm(ks[10], (DEPTH, SSM_GROUPS, SSM_GROUP, SSM_STATE), (2 * SSM_STATE) ** -0.5),
        "ssm_c_im": nrm(ks[11], (DEPTH, SSM_GROUPS, SSM_GROUP, SSM_STATE), (2 * SSM_STATE) ** -0.5),
        "ssm_d": nrm(ks[12], (DEPTH, SSM_CH), 1.0),
        "ssm_w_glu": nrm(ks[13], (DEPTH, SSM_CH, SSM_CH), SSM_CH ** -0.5),
        "mla_q_norm_g": gain(ks[14], (DEPTH, MLA_Q_RANK)),
        "mla_w_q_up": nrm(ks[15], (DEPTH, MLA_Q_RANK, MLA_HEADS * (MLA_NOPE + MLA_ROPE)), MLA_Q_RANK ** -0.5),
        "mla_kv_norm_g": gain(ks[16], (DEPTH, MLA_KV_RANK)),
        "mla_w_kv_up": nrm(ks[17], (DEPTH, MLA_KV_RANK, MLA_HEADS * (MLA_NOPE + MLA_V)), MLA_KV_RANK ** -0.5),
        "w_out": nrm(ks[18], (DEPTH, MIX_WIDTH, D_MODEL), MIX_WIDTH ** -0.5),
        "ln2_g": gain(ks[19], (DEPTH, D_MODEL)),
        "w_gate": nrm(ks[20], (DEPTH, D_MODEL, D_FF), D_MODEL ** -0.5),
        "w_up": nrm(ks[21], (DEPTH, D_MODEL, D_FF), D_MODEL ** -0.5),
        "w_down": nrm(ks[22], (DEPTH, D_FF, D_MODEL), D_FF ** -0.5),
        "final_g": gain(ks[23], (D_MODEL,)),
    }


def reference(x, positions, rel_bias, ln1_g, w_in, sinks, ssm_a_re, ssm_a_im, ssm_log_dt,
              ssm_b_re, ssm_b_im, ssm_c_re, ssm_c_im, ssm_d, ssm_w_glu,
              mla_q_norm_g, mla_w_q_up, mla_kv_norm_g, mla_w_kv_up, w_out,
              ln2_g, w_gate, w_up, w_down, final_g):
    bias = band_bias(rel_bias)
    for l in range(DEPTH):
        h = rms_norm(x, ln1_g[l])
        proj = h @ w_in[l]
        q_a, k_a, v_a, u_b, c_q, c_kv, k_r = jnp.split(proj, IN_SPLITS, axis=-1)
        o_a = swa_attention(q_a, k_a, v_a, sinks[l], bias)
        o_b = s5_mixer(u_b, ssm_a_re[l], ssm_a_im[l], ssm_log_dt[l], ssm_b_re[l], ssm_b_im[l],
                       ssm_c_re[l], ssm_c_im[l], ssm_d[l], ssm_w_glu[l])
        o_c = mla_attention(c_q, c_kv, k_r, positions, mla_q_norm_g[l], mla_w_q_up[l],
                            mla_kv_norm_g[l], mla_w_kv_up[l])
        x = x + jnp.concatenate([o_a, o_b, o_c], axis=-1) @ w_out[l]
        h = rms_norm(x, ln2_g[l])
        x = x + (jax.nn.silu(h @ w_gate[l]) * (h @ w_up[l])) @ w_down[l]
    return rms_norm(x, final_g)
```

```python
import math
from contextlib import ExitStack

import numpy as np
import concourse.bass as bass
import concourse.mybir as mybir
from concourse.bass_utils import run_bass_kernel_spmd

F32 = mybir.dt.float32
BF16 = mybir.dt.bfloat16
I32 = mybir.dt.int32
AF = mybir.ActivationFunctionType
ALU = mybir.AluOpType

S = 2048
D = 1024
NT = 16
DFF = 2816
NF = 22
EPS = 1e-6
TWO_PI = 2.0 * math.pi
PI_LO = 3.1415925


class Sem:
    def __init__(self, nc, name):
        self.h = nc.alloc_semaphore(name)
        self.count = 0


class Eng:
    def __init__(self, nc, name, e):
        self.name = name
        self.e = e
        self.sem = Sem(nc, "s_" + name)
        self.seen = {}


class KB:
    def __init__(self, nc):
        self.nc = nc
        self.PE = Eng(nc, "pe", nc.tensor)
        self.ACT = Eng(nc, "act", nc.scalar)
        self.DVE = Eng(nc, "dve", nc.vector)
        self.POOL = Eng(nc, "pool", nc.gpsimd)
        self.SP = Eng(nc, "sp", nc.sync)
        self.engs = [self.PE, self.ACT, self.DVE, self.POOL, self.SP]
        self.lw = {}
        self.rd = {}
        self.dsems = []
        self.nsem = 0

    def dsem(self, name):
        s = Sem(self.nc, "d_%s_%d" % (name, self.nsem))
        self.nsem += 1
        self.dsems.append(s)
        return s

    def _wait(self, E, need):
        for s, v in need.items():
            if E.seen.get(s, 0) < v:
                E.e.wait_ge(s.h, v)
                E.seen[s] = v

    def _need(self, E, R, W):
        need = {}
        for k in R:
            lw = self.lw.get(k)
            if lw is not None:
                need[lw[0]] = max(need.get(lw[0], 0), lw[1])
        for k in W:
            lw = self.lw.get(k)
            if lw is not None and lw[0] is not E.sem:
                need[lw[0]] = max(need.get(lw[0], 0), lw[1])
            for s, v in self.rd.get(k, {}).items():
                if s is not E.sem:
                    need[s] = max(need.get(s, 0), v)
        return need

    def _record(self, sem, val, R, W):
        for k in W:
            self.lw[k] = (sem, val)
            self.rd[k] = {}
        for k in R:
            d = self.rd.setdefault(k, {})
            d[sem] = max(d.get(sem, 0), val)

    def op(self, E, fn, R=(), W=(), sig=True):
        self._wait(E, self._need(E, R, W))
        ins = fn()
        if sig:
            E.sem.count += 1
            ins.then_inc(E.sem.h, 1)
            val = E.sem.count
        else:
            val = E.sem.count + 1
        self._record(E.sem, val, R, W)
        return ins

    def mm(self, R, W, mms):
        E = self.PE
        self._wait(E, self._need(E, R, W))
        n = len(mms)
        for i, (o, l, r, st, sp) in enumerate(mms):
            ins = E.e.matmul(o, lhsT=l, rhs=r, start=st, stop=sp)
            if i == n - 1:
                E.sem.count += 1
                ins.then_inc(E.sem.h, 1)
        self._record(E.sem, E.sem.count, R, W)

    def tr(self, R, W, trs, ident):
        E = self.PE
        self._wait(E, self._need(E, R, W))
        n = len(trs)
        for i, (o, a) in enumerate(trs):
            ins = E.e.transpose(out=o, in_=a, identity=ident)
            if i == n - 1:
                E.sem.count += 1
                ins.then_inc(E.sem.h, 1)
        self._record(E.sem, E.sem.count, R, W)

    def dma(self, Q, out, in_, R, W, sem):
        self._wait(Q, self._need(Q, R, W))
        ins = Q.e.dma_start(out=out, in_=in_)
        sem.count += 16
        ins.then_inc(sem.h, 16)
        self._record(sem, sem.count, R, W)

    def snapshot(self):
        snap = {}
        for F in self.engs:
            if F.sem.count > 0:
                snap[F.sem] = F.sem.count
        for s_ in self.dsems:
            if s_.count > 0:
                snap[s_] = s_.count
        return snap

    def barrier(self, snap=None, keep=None):
        if snap is None:
            snap = self.snapshot()
        for E in self.engs:
            self._wait(E, {s_: v for s_, v in snap.items() if s_ is not E.sem})
        if keep is None:
            self.lw = {}
            self.rd = {}
        else:
            self.lw = {k: v for k, v in self.lw.items() if keep(k)}
            self.rd = {k: v for k, v in self.rd.items() if keep(k)}


def t5_bucket_np():
    d = np.arange(128)
    n = np.maximum(d, 1).astype(np.float32)
    large = 16 + (np.log(n / np.float32(16)) / np.float32(math.log(128 / 16)) * np.float32(16)).astype(np.int32)
    large = np.minimum(large, 31)
    return np.where(d < 16, d, large)


def build(nlayers=2, dbg=None):
    nc = bass.Bass("TRN2", target_bir_lowering=False)
    dt = nc.dram_tensor
    x_d = dt("x", [S, D], F32, kind="ExternalInput").ap()
    pos_d = dt("positions", [S], I32, kind="ExternalInput").ap()
    relb_d = dt("rel_bias", [32, 8], F32, kind="ExternalInput").ap()
    ln1_d = dt("ln1_g", [2, D], F32, kind="ExternalInput").ap()
    win_d = dt("w_in", [2, D, 1440], F32, kind="ExternalInput").ap()
    sinks_d = dt("sinks", [2, 8], F32, kind="ExternalInput").ap()
    are_d = dt("ssm_a_re", [2, 16, 64], F32, kind="ExternalInput").ap()
    aim_d = dt("ssm_a_im", [2, 16, 64], F32, kind="ExternalInput").ap()
    ldt_d = dt("ssm_log_dt", [2, 16], F32, kind="ExternalInput").ap()
    bre_d = dt("ssm_b_re", [2, 16, 64, 16], F32, kind="ExternalInput").ap()
    bim_d = dt("ssm_b_im", [2, 16, 64, 16], F32, kind="ExternalInput").ap()
    cre_d = dt("ssm_c_re", [2, 16, 16, 64], F32, kind="ExternalInput").ap()
    cim_d = dt("ssm_c_im", [2, 16, 16, 64], F32, kind="ExternalInput").ap()
    sd_d = dt("ssm_d", [2, 256], F32, kind="ExternalInput").ap()
    wglu_d = dt("ssm_w_glu", [2, 256, 256], F32, kind="ExternalInput").ap()
    gq_d = dt("mla_q_norm_g", [2, 256], F32, kind="ExternalInput").ap()
    wq_d = dt("mla_w_q_up", [2, 256, 384], F32, kind="ExternalInput").ap()
    gkv_d = dt("mla_kv_norm_g", [2, 128], F32, kind="ExternalInput").ap()
    wkv_d = dt("mla_w_kv_up", [2, 128, 512], F32, kind="ExternalInput").ap()
    wout_d = dt("w_out", [2, D, D], F32, kind="ExternalInput").ap()
    ln2_d = dt("ln2_g", [2, D], F32, kind="ExternalInput").ap()
    wg_d = dt("w_gate", [2, D, DFF], F32, kind="ExternalInput").ap()
    wu_d = dt("w_up", [2, D, DFF], F32, kind="ExternalInput").ap()
    wd_d = dt("w_down", [2, DFF, D], F32, kind="ExternalInput").ap()
    fg_d = dt("final_g", [D], F32, kind="ExternalInput").ap()
    out_d = dt("out", [S, D], F32, kind="ExternalOutput").ap()
    gd_d = dt("gd_scratch", [2, 8, 128, 256], F32, kind="Internal").ap()
    rope_d = dt("rope_scratch", [2, 32, S], F32, kind="Internal").ap()
    dbg_d = {}
    if dbg:
        for nm, shp in dbg.items():
            if nm.startswith("_"):
                continue
            dbg_d[nm] = dt("dbg_" + nm, shp, F32, kind="ExternalOutput").ap()

    K = KB(nc)
    PE, ACT, DVE, POOL, SP = K.PE, K.ACT, K.DVE, K.POOL, K.SP
    _cnt = [0]

    def ST(nm, shp, dtp):
        _cnt[0] += 1
        return nc.sbuf_tensor("%s_u%d" % (nm, _cnt[0]), shp, dtp)

    A = nc.alloc_sbuf_tensor

    es0 = ExitStack()
    ps = es0.enter_context(nc.psum_tensor("ps", [128, 8, 512], F32))
    psT = ps[:, 6:8, :].bitcast(BF16)
    nc_ctx = es0.enter_context(nc.allow_non_contiguous_dma(reason="small param layouts"))

    x_sb = A("x_sb", [128, NT, D], F32)
    NRING = 12
    ring = [A("ring%d" % i, [128, 1024], BF16) for i in range(NRING)]
    rsem = [K.dsem("ring") for _ in range(NRING)]
    ident = A("ident", [128, 128], BF16)
    ones = A("ones", [128, 128], BF16)
    iota_i = A("iota_i", [128, 129], F32)
    pidx = A("pidx", [128, 1], F32)
    invf = A("invf", [128, 1], F32)
    EB = A("EB", [128, 2, 8, 128], F32)
    ss = A("ss", [128, NT], F32)
    rstd = A("rstd", [128, NT], F32)
    gcol = A("gcol", [128, 4, 8], F32)
    ring_state = {"next": 0}

    def ring_alloc():
        i = ring_state["next"] % NRING
        ring_state["next"] += 1
        return i

    def wload(src_ap, slot, view=None):
        o = ring[slot][:] if view is None else view
        K.dma(POOL, o, src_ap, R=(), W=[("ring", slot)], sem=rsem[slot])

    psem = K.dsem("params")
    for l in range(2):
        K.dma(SP, gcol[:, 2 * l, :], ln1_d[l].rearrange("(k p) -> p k", p=128), R=(), W=["gcol"], sem=psem)
        K.dma(SP, gcol[:, 2 * l + 1, :], ln2_d[l].rearrange("(k p) -> p k", p=128), R=(), W=["gcol"], sem=psem)
    es_setup = ExitStack()
    rb = es_setup.enter_context(ST("rb", [128, 32, 8], F32))
    rposi = es_setup.enter_context(ST("rposi", [128, 4, 512], I32))
    K.dma(SP, rb[:].rearrange("p a b -> p (a b)"), relb_d.rearrange("a b -> (a b)").partition_broadcast(128), R=(), W=["rb"], sem=K.dsem("rb"))
    K.dma(SP, rposi[64:96, :, :].rearrange("p a b -> p (a b)"), pos_d.partition_broadcast(32), R=(), W=["rposi"], sem=K.dsem("rpos"))
    xr = x_d.rearrange("(t p) d -> p t d", p=128)
    for c in range(4):
        K.dma(SP, x_sb[:, 4 * c:4 * c + 4, :], xr[:, 4 * c:4 * c + 4, :], R=(), W=[("x", t) for t in range(4 * c, 4 * c + 4)], sem=K.dsem("x%d" % c))

    def load_w_in(l, scale=True, dma=True, seng=None):
        wv = win_d[l].rearrange("(k p) c -> p k c", p=128)
        g1 = gcol[:, 2 * l, :].unsqueeze(2).broadcast_to([128, 8, 128])
        order = [7, 8, 9, 10, 11, 0, 1, 2, 3, 4, 5, 6]
        if dma:
            for j in order:
                sl = j
                rv = ring[sl][:].rearrange("p (k c) -> p k c", k=8)
                if j < 4:
                    wload(wv[:, :, j * 64:(j + 1) * 64], sl, rv[:, :, 0:64])
                    wload(wv[:, :, (4 + j) * 64:(5 + j) * 64], sl, rv[:, :, 64:128])
                elif j < 11:
                    c0 = 512 + (j - 4) * 128
                    wload(wv[:, :, c0:c0 + 128], sl, rv)
                else:
                    wload(wv[:, :, 1344:1408], sl, rv[:, :, 0:64])
                    wload(wv[:, :, 1408:1440], sl, rv[:, :, 64:96])
                    wload(wv[:, :, 1424:1440], sl, rv[:, :, 96:112])
                    wload(wv[:, :, 1408:1424], sl, rv[:, :, 112:128])
        if scale:
            for j in order:
                sl = j
                rv = ring[sl][:].rearrange("p (k c) -> p k c", k=8)
                SE = seng if seng is not None else POOL
                if j == 11:
                    K.op(SE, lambda rv=rv: SE.e.tensor_scalar(out=rv[:, :, 96:112], in0=rv[:, :, 96:112], scalar1=-1.0, scalar2=None, op0=ALU.mult), R=[("ring", sl)], W=[("ring", sl)])
                K.op(SE, lambda rv=rv: SE.e.tensor_tensor(out=rv, in0=rv, in1=g1, op=ALU.mult), R=[("ring", sl), "gcol"], W=[("ring", sl)])

    isring = lambda k: isinstance(k, tuple) and k[0] == "ring"
    with es_setup as es:
        identf = es.enter_context(ST("identf", [128, 128], F32))
        G = es.enter_context(ST("G", [128, 2, 8, 256], F32))
        K.op(POOL, lambda: POOL.e.memset(identf[:], 0.0), W=["identf"])
        K.op(POOL, lambda: POOL.e.affine_select(out=identf[:], in_=identf[:], pattern=[[-1, 128]], compare_op=ALU.not_equal, fill=1.0, base=0, channel_multiplier=1), R=["identf"], W=["identf"])
        K.op(POOL, lambda: POOL.e.iota(iota_i[:], pattern=[[1, 129]], base=0, channel_multiplier=0, allow_small_or_imprecise_dtypes=True), W=["iota_i"])
        K.op(POOL, lambda: POOL.e.iota(pidx[:], pattern=[[0, 1]], base=0, channel_multiplier=1, allow_small_or_imprecise_dtypes=True), W=["pidx"])
        K.op(POOL, lambda: POOL.e.memset(G[:].rearrange("p a b c -> p (a b c)"), 0.0), W=["G"])
        wstg = es.enter_context(ST("wstg", [128, 8, 1440], F32))
        wsem = K.dsem("wstg")
        wv0 = win_d[0].rearrange("(k p) c -> p k c", p=128)
        for k2 in range(4):
            K.dma(SP, wstg[:, 2 * k2:2 * k2 + 2, :], wv0[:, 2 * k2:2 * k2 + 2, :], R=(), W=["wstg"], sem=wsem)
        K.op(DVE, lambda: DVE.e.tensor_copy(out=ident[:], in_=identf[:]), R=["identf"], W=["ident"])
        K.op(DVE, lambda: DVE.e.memset(ones[:], 1.0), W=["ones"])
        K.op(DVE, lambda: DVE.e.tensor_scalar(out=invf[:], in0=pidx[:], scalar1=1.0 / 16.0, scalar2=None, op0=ALU.mult), R=["pidx"], W=["invf"])
        kki = es.enter_context(ST("kki", [128, 1], I32))
        kkf = es.enter_context(ST("kkf", [128, 1], F32))
        K.op(DVE, lambda: DVE.e.tensor_scalar(out=kkf[:], in0=invf[:], scalar1=-0.46875, scalar2=None, op0=ALU.add), R=["invf"], W=["kkf"])
        K.op(DVE, lambda: DVE.e.tensor_copy(out=kki[:], in_=kkf[:]), R=["kkf"], W=["kki"])
        K.op(DVE, lambda: DVE.e.tensor_copy(out=kkf[:], in_=kki[:]), R=["kki"], W=["kkf"])
        K.op(DVE, lambda: DVE.e.scalar_tensor_tensor(out=invf[:], in0=kkf[:], scalar=-16.0, in1=pidx[:], op0=ALU.mult, op1=ALU.add), R=["kkf", "pidx"], W=["invf"])
        K.op(ACT, lambda: ACT.e.activation(out=invf[:], in_=invf[:], func=AF.Exp, scale=-math.log(10000.0) / 16.0), R=["invf"], W=["invf"])
        bk = t5_bucket_np()
        runs = []
        d0 = 0
        for d in range(1, 129):
            if d == 128 or bk[d] != bk[d0]:
                runs.append((d0, d, int(bk[d0])))
                d0 = d
        gk = []
        for (lo, hi, b) in runs:
            src = rb[:, b, :].unsqueeze(2).broadcast_to([128, 8, hi - lo])
            lo_p = 1 if lo == 0 else lo
            if hi > lo_p:
                srcp = rb[:, b, :].unsqueeze(2).broadcast_to([128, 8, hi - lo_p])
                key = ("G", len(gk)); gk.append(key)
                K.op(DVE, lambda srcp=srcp, lo_p=lo_p, hi=hi: DVE.e.tensor_copy(out=G[:, 0, :, lo_p:hi], in_=srcp), R=["rb", "G"], W=[key])
            key = ("G", len(gk)); gk.append(key)
            K.op(DVE, lambda src=src, lo=lo, hi=hi: DVE.e.tensor_copy(out=G[:, 1, :, 128 + lo:128 + hi], in_=src), R=["rb", "G"], W=[key])
        K.op(ACT, lambda: ACT.e.activation(out=G[:, 0, :, 1:128], in_=G[:, 0, :, 1:128], func=AF.Exp), R=["G"] + gk, W=["G"])
        K.op(ACT, lambda: ACT.e.activation(out=G[:, 1, :, 128:256], in_=G[:, 1, :, 128:256], func=AF.Exp), R=["G"] + gk, W=["G"])
        R64s = slice(64, 96)
        rposf = es.enter_context(ST("rposf", [128, 512], F32)); rang = es.enter_context(ST("rang", [128, 512], F32))
        rr1 = es.enter_context(ST("rr1", [128, 512], F32)); rki = es.enter_context(ST("rki", [128, 512], I32))
        rtab = [es.enter_context(ST("rtab%d" % i, [128, 4, 512], F32)) for i in range(2)]
        rsem2 = K.dsem("ropeout")
        for c in range(4):
            K.op(DVE, lambda c=c: DVE.e.tensor_copy(out=rposf[R64s, :], in_=rposi[R64s, c, :]), R=["rposi"], W=["rposf"])
            K.op(DVE, lambda: DVE.e.tensor_scalar(out=rang[R64s, :], in0=rposf[R64s, :], scalar1=invf[R64s, 0:1], scalar2=None, op0=ALU.mult), R=["rposf", "invf"], W=["rang"])
            for kind_, shift in ((0, 0.0), (1, math.pi / 2)):
                K.op(DVE, lambda shift=shift: DVE.e.tensor_scalar(out=rr1[R64s, :], in0=rang[R64s, :], scalar1=1.0 / TWO_PI, scalar2=shift / TWO_PI, op0=ALU.mult, op1=ALU.add), R=["rang"], W=["rr1"])
                K.op(DVE, lambda: DVE.e.tensor_copy(out=rki[R64s, :], in_=rr1[R64s, :]), R=["rr1"], W=["rki"])
                K.op(DVE, lambda: DVE.e.tensor_copy(out=rr1[R64s, :], in_=rki[R64s, :]), R=["rki"], W=["rr1"])
                K.op(DVE, lambda: DVE.e.scalar_tensor_tensor(out=rr1[R64s, :], in0=rr1[R64s, :], scalar=-TWO_PI, in1=rang[R64s, :], op0=ALU.mult, op1=ALU.add), R=["rr1", "rang"], W=["rr1"])
                K.op(DVE, lambda shift=shift: DVE.e.tensor_scalar(out=rr1[R64s, :], in0=rr1[R64s, :], scalar1=shift, scalar2=PI_LO, op0=ALU.add, op1=ALU.min), R=["rr1"], W=["rr1"])
                K.op(DVE, lambda: DVE.e.tensor_scalar(out=rr1[R64s, :], in0=rr1[R64s, :], scalar1=-PI_LO, scalar2=None, op0=ALU.max), R=["rr1"], W=["rr1"])
                K.op(ACT, lambda kind_=kind_, c=c: ACT.e.activation(out=rtab[kind_][R64s, c, :], in_=rr1[R64s, :], func=AF.Sin), R=["rr1"], W=[("rtab", kind_, c)])
                K.dma(ACT, rope_d[kind_, :, c * 512:(c + 1) * 512], rtab[kind_][R64s, c, :], R=[("rtab", kind_, c)], W=["rope_d"], sem=rsem2)
        gsem = K.dsem("gd")
        for kind in range(2):
            K.dma(ACT, gd_d[kind].rearrange("h p m -> p h m"), G[:, kind, :, :], R=["G"], W=["gd"], sem=gsem)
        gsem2 = K.dsem("gd2")
        for kind in range(2):
            for h in range(8):
                off = ((kind * 8 + h) * 128) * 256 + 128
                skew = bass.AP(gd_d.tensor, off, [[255, 128], [1, 128]])
                K.dma(ACT, EB[:, kind, h, :], skew, R=["gd"], W=["EB"], sem=gsem2)
        g1_ = lambda n_: gcol[:, 0, :].unsqueeze(2).broadcast_to([128, 8, n_])
        ci = 0

        def wcast(dst, c0, n_, neg=False):
            nonlocal ci
            E_ = DVE if (ci % 2 == 0 or ci == 11) else POOL
            if neg:
                K.op(E_, lambda: E_.e.scalar_tensor_tensor(out=dst, in0=wstg[:, :, c0:c0 + n_], scalar=-1.0, in1=g1_(n_), op0=ALU.mult, op1=ALU.mult), R=["wstg", "gcol"], W=[("ring", -1)])
            else:
                K.op(E_, lambda: E_.e.tensor_tensor(out=dst, in0=wstg[:, :, c0:c0 + n_], in1=g1_(n_), op=ALU.mult), R=["wstg", "gcol"], W=[("ring", -1)])

        for j in range(12):
            rv = ring[j][:].rearrange("p (k c) -> p k c", k=8)
            if j < 4:
                wcast(rv[:, :, 0:64], j * 64, 64)
                wcast(rv[:, :, 64:128], (4 + j) * 64, 64)
            elif j < 11:
                wcast(rv, 512 + (j - 4) * 128, 128)
            else:
                wcast(rv[:, :, 0:64], 1344, 64)
                wcast(rv[:, :, 64:96], 1408, 32)
                wcast(rv[:, :, 96:112], 1424, 16, neg=True)
                wcast(rv[:, :, 112:128], 1408, 16)
            K.lw[("ring", j)] = K.lw[("ring", -1)]
            ci += 1
        snap0 = K.snapshot()
        K.barrier(snap0, keep=isring)

    def rstd_from_ss(ss_ap, rstd_ap, n, dim, keyR, keyW, tmpk):
        K.op(ACT, lambda: ACT.e.activation(out=rstd_ap, in_=ss_ap, func=AF.Ln, scale=1.0 / dim, bias=EPS), R=keyR, W=keyW)
        K.op(ACT, lambda: ACT.e.activation(out=rstd_ap, in_=rstd_ap, func=AF.Exp, scale=-0.5), R=keyW, W=keyW)

    evac_flip = {"n": 0}

    def evac(out_ap, in_ap, R, W, eng=None):
        if eng is None:
            eng = ACT if (evac_flip["n"] % 2 == 0) else DVE
            evac_flip["n"] += 1
        if eng is ACT:
            K.op(ACT, lambda: ACT.e.activation(out=out_ap, in_=in_ap, func=AF.Copy), R=R, W=W)
        else:
            K.op(DVE, lambda: DVE.e.tensor_copy(out=out_ap, in_=in_ap), R=R, W=W)

    bank = {"n": 0}

    def nbank():
        b = bank["n"] % 6
        bank["n"] += 1
        return b

    def norm_tiles(tiles, hT, hkey, junk, hb, stats=True):
        t0_, t1_ = tiles[0], tiles[-1] + 1
        if stats:
            for t in tiles:
                K.op(ACT, lambda t=t: ACT.e.activation(out=junk[:], in_=x_sb[:, t, :], func=AF.Square, accum_out=ss[:, t:t + 1]), R=[("x", t)], W=["junk", ("ss", t)])
            rstd_from_ss(ss[:, t0_:t1_], rstd[:, t0_:t1_], len(tiles), D, [("ss", t) for t in tiles], [("rstd", t) for t in tiles], None)
        for i, t in enumerate(tiles):
            hbuf = hb[i % 2]
            K.op(DVE, lambda t=t, hbuf=hbuf: DVE.e.tensor_scalar(out=hbuf[:], in0=x_sb[:, t, :], scalar1=rstd[:, t:t + 1], scalar2=None, op0=ALU.mult), R=[("x", t), ("rstd", t)], W=[("hb", i % 2)])
            pb = i % 2
            K.tr(R=[("hb", i % 2), "ident"], W=[("psT", pb)], trs=[(psT[:, pb, k * 128:(k + 1) * 128], hbuf[:, k * 128:(k + 1) * 128]) for k in range(8)], ident=ident[:])
            dst = hT(i)
            evac(dst, psT[:, pb, :].rearrange("p (k c) -> p k c", k=8), R=[("psT", pb)], W=[hkey(i)])

    def dbg_dump(name, src_ap, R):
        if dbg and name in dbg_d:
            s_ = K.dsem("dbg")
            K.dma(SP, dbg_d[name], src_ap, R=R, W=(), sem=s_)

    def dump_bf(name, ap_bf, shape, R, es_):
        if not (dbg and name in dbg_d):
            return
        tmp = es_.enter_context(ST("dbgt_" + name, shape, F32))
        K.op(DVE, lambda: DVE.e.tensor_copy(out=tmp[:], in_=ap_bf), R=R, W=["dbgt_" + name])
        s_ = K.dsem("dbg")
        K.dma(SP, dbg_d[name], tmp[:], R=["dbgt_" + name], W=(), sem=s_)

    for l in range(nlayers):
        esL = ExitStack()
        AL = lambda nm, shp, dtp: esL.enter_context(ST("%s_l%d" % (nm, l), shp, dtp))
        qT = AL("qT", [128, 4, S], BF16)
        uT = AL("uT", [128, 2, S], BF16)
        cqT = AL("cqT", [128, 2, S], BF16)
        ckvT = AL("ckvT", [128, S], BF16)
        krT = AL("krT", [128, S], BF16)
        krotT = AL("krotT", [128, S], BF16)
        es_bc = AL("es_bc", [128, 8], F32)
        lps = K.dsem("lparams")
        K.dma(SP, es_bc[:], sinks_d[l].partition_broadcast(128), R=(), W=["es_bc"], sem=lps)
        K.op(ACT, lambda: ACT.e.activation(out=es_bc[:], in_=es_bc[:], func=AF.Exp), R=["es_bc"], W=["es_bc"])
        dcol = AL("dcol", [128, 2], F32)
        wglu = AL("wglu", [128, 2, 256], BF16)
        are = AL("are", [128, 8], F32); aim = AL("aim", [128, 8], F32); ldt = AL("ldt", [128, 8], F32)
        Bre = AL("Bre", [128, 8, 16], F32); Bim = AL("Bim", [128, 8, 16], F32)
        Cre = AL("Cre", [128, 8, 16], F32); Cim = AL("Cim", [128, 8, 16], F32)
        sps = K.dsem("ssmp")
        for gl in range(2):
            pr = slice(gl * 64, (gl + 1) * 64)
            K.dma(SP, are[pr, :], bass.AP(are_d.tensor, (l * 16 + gl) * 64, [[1, 64], [128, 8]]), R=(), W=["ssmp"], sem=sps)
            K.dma(SP, aim[pr, :], bass.AP(aim_d.tensor, (l * 16 + gl) * 64, [[1, 64], [128, 8]]), R=(), W=["ssmp"], sem=sps)
            K.dma(SP, ldt[pr, :], bass.AP(ldt_d.tensor, l * 16 + gl, [[0, 64], [2, 8]]), R=(), W=["ssmp"], sem=sps)
            K.dma(SP, Bre[pr, :, :], bass.AP(bre_d.tensor, (l * 16 + gl) * 1024, [[16, 64], [2048, 8], [1, 16]]), R=(), W=["ssmp"], sem=sps)
            K.dma(SP, Bim[pr, :, :], bass.AP(bim_d.tensor, (l * 16 + gl) * 1024, [[16, 64], [2048, 8], [1, 16]]), R=(), W=["ssmp"], sem=sps)
            for tp in range(8):
                K.dma(SP, Cre[pr, tp, :], bass.AP(cre_d.tensor, (l * 16 + 2 * tp + gl) * 1024, [[1, 64], [64, 16]]), R=(), W=["ssmp"], sem=sps)
                K.dma(SP, Cim[pr, tp, :], bass.AP(cim_d.tensor, (l * 16 + 2 * tp + gl) * 1024, [[1, 64], [64, 16]]), R=(), W=["ssmp"], sem=sps)
        K.dma(SP, dcol[:], sd_d[l].rearrange("(h p) -> p h", p=128), R=(), W=["ssmp"], sem=sps)
        K.dma(POOL, wglu[:], wglu_d[l].rearrange("(k p) c -> p k c", p=128), R=(), W=["wglu"], sem=K.dsem("wglu"))


        wq = AL("wq", [128, 2, 384], BF16)
        wkv = AL("wkv", [128, 512], BF16)
        gqc = AL("gqc", [128, 2], F32); gkvc = AL("gkvc", [128, 1], F32)
        mps = K.dsem("mlap")
        K.dma(SP, gqc[:], gq_d[l].rearrange("(k p) -> p k", p=128), R=(), W=["mlap"], sem=mps)
        K.dma(SP, gkvc[:], gkv_d[l].rearrange("(k p) -> p k", p=128), R=(), W=["mlap"], sem=mps)
        mps2 = K.dsem("mlaw")
        K.dma(POOL, wq[:], wq_d[l].rearrange("(k p) c -> p k c", p=128), R=(), W=["wq"], sem=mps2)
        K.dma(POOL, wkv[:], wkv_d[l], R=(), W=["wkv"], sem=K.dsem("mlaw2"))
        K.op(POOL, lambda: POOL.e.tensor_tensor(out=wq[:], in0=wq[:], in1=gqc[:].unsqueeze(2).broadcast_to([128, 2, 384]), op=ALU.mult), R=["wq", "mlap"], W=["wq"])
        K.op(POOL, lambda: POOL.e.tensor_scalar(out=wkv[:], in0=wkv[:], scalar1=gkvc[:, 0:1], scalar2=None, op0=ALU.mult), R=["wkv", "mlap"], W=["wkv"])
        esS = ExitStack()
        kT = esS.enter_context(ST("kT_l%d" % l, [128, S], BF16))
        v_a = esS.enter_context(ST("v_a_l%d" % l, [128, NT, 128], BF16))
        with ExitStack() as es:
            hTb = [es.enter_context(ST("hTb%d" % i, [128, 8, 512], BF16)) for i in range(2)]
            junk = es.enter_context(ST("junk", [128, D], BF16))
            hb = [es.enter_context(ST("hb%d" % i, [128, D], BF16)) for i in range(2)]
            units = list(range(12))
            def stats_chunk(c):
                tl = list(range(4 * c, 4 * c + 4))
                for t in tl:
                    K.op(ACT, lambda t=t: ACT.e.activation(out=junk[:], in_=x_sb[:, t, :], func=AF.Square, accum_out=ss[:, t:t + 1]), R=[("x", t)], W=["junk", ("ss", t)])
                rstd_from_ss(ss[:, 4 * c:4 * c + 4], rstd[:, 4 * c:4 * c + 4], 4, D, [("ss", t) for t in tl], [("rstd", t) for t in tl], None)

            def norm_chunk(c):
                hT_ = hTb[c % 2]
                norm_tiles(list(range(4 * c, 4 * c + 4)), lambda i, hT_=hT_: hT_[:, :, i * 128:(i + 1) * 128], lambda i, c=c: ("hT", c % 2, i), junk, hb, stats=False)

            stats_chunk(0)
            stats_chunk(1)
            norm_chunk(0)
            for c in range(4):
                hT = hTb[c % 2]
                if c + 2 < 4:
                    stats_chunk(c + 2)
                if c + 1 < 4:
                    norm_chunk(c + 1)
                hR = [("hT", c % 2, i) for i in range(4)]
                tok = slice(c * 512, (c + 1) * 512)
                for j in range(12):
                    sl = units[j]
                    rv = ring[sl][:].rearrange("p (k c) -> p k c", k=8)
                    if j == 5:
                        b = nbank()
                        for i in range(4):
                            K.mm(R=hR + [("ring", sl)], W=[("ps", b)], mms=[(ps[:, b, i * 128:(i + 1) * 128], hT[:, k, i * 128:(i + 1) * 128], rv[:, k, :], k == 0, k == 7) for k in range(8)])
                        evac(v_a[:, 4 * c:4 * c + 4, :], ps[:, b, :].rearrange("p (a b) -> p a b", a=4), R=[("ps", b)], W=[("v_a", c)])
                        continue
                    if j == 11:
                        for (dst, c0, key) in ((krT, 0, "krT"), (krotT, 32, "krotT")):
                            b = nbank()
                            K.mm(R=hR + [("ring", sl)], W=[("ps", b)], mms=[(ps[0:96, b, :], rv[:, k, c0:c0 + 96], hT[:, k, :], k == 0, k == 7) for k in range(8)])
                            evac(dst[64:96, tok], ps[64:96, b, :], R=[("ps", b)], W=[(key, c)])
                        continue
                    b = nbank()
                    K.mm(R=hR + [("ring", sl)], W=[("ps", b)], mms=[(ps[:, b, :], rv[:, k, :], hT[:, k, :], k == 0, k == 7) for k in range(8)])
                    if j < 4:
                        dst, key = qT[:, j, tok], ("qT", j, c)
                    elif j == 4:
                        dst, key = kT[:, tok], ("kT", c)
                    elif j < 8:
                        dst, key = uT[:, j - 6, tok], ("uT", j - 6, c)
                    elif j < 10:
                        dst, key = cqT[:, j - 8, tok], ("cqT", j - 8, c)
                    else:
                        dst, key = ckvT[:, tok], ("ckvT", c)
                    evac(dst, ps[:, b, :], R=[("ps", b)], W=[key])
            K.barrier()
            if dbg and l == dbg.get("_layer", 0):
                dump_bf("qT", qT[:].rearrange("p a s -> p (a s)"), [128, 4 * S], [], es)
                dump_bf("kT", kT[:], [128, S], [], es)
                dump_bf("v_a", v_a[:].rearrange("p a s -> p (a s)"), [128, NT * 128], [], es)
                dump_bf("uT", uT[:].rearrange("p a s -> p (a s)"), [128, 2 * S], [], es)
                dump_bf("cqT", cqT[:].rearrange("p a s -> p (a s)"), [128, 2 * S], [], es)
                dump_bf("ckvT", ckvT[:], [128, S], [], es)
                dump_bf("krT", krT[64:96, :], [32, S], [], es)
                dump_bf("krotT", krotT[64:96, :], [32, S], [], es)
                K.barrier()

        wo_units = []
        for k in range(8):
            sl = k
            wo_units.append(sl)
            if k < 4:
                wload(wout_d[l, k * 64:(k + 1) * 64, :], sl, ring[sl][0:64, :])
                wload(wout_d[l, (4 + k) * 64:(5 + k) * 64, :], sl, ring[sl][64:128, :])
            else:
                wload(wout_d[l, k * 128:(k + 1) * 128, :], sl)

        with ExitStack() as es:
            T_ = lambda nm, shp, dtp=F32: es.enter_context(ST(nm, shp, dtp))
            rr = T_("rr", [128, 8]); cL = T_("cL", [128, 8]); sL = T_("sL", [128, 8]); nsL = T_("nsL", [128, 8])
            cosT = T_("cosT", [128, 8, 129]); sinT = T_("sinT", [128, 8, 129]); nsinT = T_("nsinT", [128, 8, 128])
            LBr = T_("LBr", [128, 8, 128], BF16); LBi = T_("LBi", [128, 8, 128], BF16)
            LCr = T_("LCr", [128, 8, 128], BF16); LCi = T_("LCi", [128, 8, 128], BF16)
            rec = []
            _real = (K.op, K.tr, K.mm, K.dma)
            K.op = lambda *a_, **k_: rec.append((_real[0], a_, k_))
            K.tr = lambda *a_, **k_: rec.append((_real[1], a_, k_))
            K.mm = lambda *a_, **k_: rec.append((_real[2], a_, k_))
            K.dma = lambda *a_, **k_: rec.append((_real[3], a_, k_))
            est = ExitStack()
            Tt = lambda nm, shp, dtp=F32: est.enter_context(ST(nm, shp, dtp))
            dtt = Tt("dtt", [128, 8]); ar = Tt("ar", [128, 8]); th = Tt("th", [128, 8])
            cth = Tt("cth", [128, 8]); sth = Tt("sth", [128, 8]); t8a = Tt("t8a", [128, 8]); t8b = Tt("t8b", [128, 8])
            t8i = Tt("t8i", [128, 8], I32)
            kr_ = Tt("kr_", [128, 8]); ki_ = Tt("ki_", [128, 8]); den = Tt("den", [128, 8])
            angT = Tt("angT", [128, 8, 129]); angI = Tt("angI", [128, 8, 129], I32); tmpT = Tt("tmpT", [128, 8, 129])
            Bbr = Tt("Bbr", [128, 8, 16]); Bbi = Tt("Bbi", [128, 8, 16]); tB = Tt("tB", [128, 8, 16])
            BX = Tt("BX", [128, 8, 128], BF16)
            V = DVE

            def dv(fn, R, W):
                K.op(DVE, fn, R=R, W=W)

            def sincos(ang, out_s, out_c, tmpf, tmpi, keyp):
                for (dst, shift) in ((out_s, 0.0), (out_c, math.pi / 2)):
                    dv(lambda shift=shift: V.e.tensor_scalar(out=tmpf, in0=ang, scalar1=1.0 / TWO_PI, scalar2=shift / TWO_PI, op0=ALU.mult, op1=ALU.add), [keyp + "ang"], [keyp + "tf"])
                    dv(lambda: V.e.tensor_copy(out=tmpi, in_=tmpf), [keyp + "tf"], [keyp + "ti"])
                    dv(lambda: V.e.tensor_copy(out=tmpf, in_=tmpi), [keyp + "ti"], [keyp + "tf"])
                    dv(lambda: V.e.scalar_tensor_tensor(out=tmpf, in0=tmpf, scalar=-TWO_PI, in1=ang, op0=ALU.mult, op1=ALU.add), [keyp + "tf", keyp + "ang"], [keyp + "tf"])
                    dv(lambda shift=shift: V.e.tensor_scalar(out=tmpf, in0=tmpf, scalar1=shift, scalar2=PI_LO, op0=ALU.add, op1=ALU.min), [keyp + "tf"], [keyp + "tf"])
                    dv(lambda: V.e.tensor_scalar(out=tmpf, in0=tmpf, scalar1=-PI_LO, scalar2=None, op0=ALU.max), [keyp + "tf"], [keyp + "tf"])
                    K.op(ACT, lambda dst=dst: ACT.e.activation(out=dst, in_=tmpf, func=AF.Sin), R=[keyp + "tf"], W=[keyp + "o" + str(shift)])

            P_ = ["ssmp"]
            K.op(ACT, lambda: ACT.e.activation(out=dtt[:], in_=ldt[:], func=AF.Exp), R=P_, W=["dtt"])
            dv(lambda: V.e.tensor_tensor(out=ar[:], in0=are[:], in1=dtt[:], op=ALU.mult), P_ + ["dtt"], ["ar"])
            dv(lambda: V.e.tensor_tensor(out=th[:], in0=aim[:], in1=dtt[:], op=ALU.mult), P_ + ["dtt"], ["s8ang"])
            K.op(ACT, lambda: ACT.e.activation(out=rr[:], in_=ar[:], func=AF.Exp), R=["ar"], W=["rr"])
            dv(lambda: V.e.tensor_tensor(out=angT[:], in0=th[:].unsqueeze(2).broadcast_to([128, 8, 129]), in1=iota_i[:].unsqueeze(1).broadcast_to([128, 8, 129]), op=ALU.mult), ["s8ang", "iota_i"], ["Tang"])
            sincos(angT[:], sinT[:], cosT[:], tmpT[:], angI[:], "T")
            TT = ["To0.0", "To" + str(math.pi / 2)]
            S8 = ["sth", "cth"]
            CL = ["sL", "cL"]
            dv(lambda: V.e.tensor_copy(out=sth[:], in_=sinT[:, :, 1]), TT, ["sth"])
            dv(lambda: V.e.tensor_copy(out=cth[:], in_=cosT[:, :, 1]), TT, ["cth"])
            dv(lambda: V.e.tensor_copy(out=sL[:], in_=sinT[:, :, 128]), TT, ["sL"])
            dv(lambda: V.e.tensor_copy(out=cL[:], in_=cosT[:, :, 128]), TT, ["cL"])
            dv(lambda: V.e.tensor_scalar(out=nsL[:], in0=sL[:], scalar1=-1.0, scalar2=None, op0=ALU.mult), ["sL"], ["nsL"])
            dv(lambda: V.e.tensor_scalar(out=nsinT[:], in0=sinT[:, :, 0:128], scalar1=-1.0, scalar2=None, op0=ALU.mult), TT, ["nsinT"])
            dv(lambda: V.e.tensor_tensor(out=t8a[:], in0=rr[:], in1=cth[:], op=ALU.mult), ["rr"] + S8, ["s8tf"])
            dv(lambda: V.e.tensor_scalar(out=t8a[:], in0=t8a[:], scalar1=-1.0, scalar2=None, op0=ALU.add), ["s8tf"], ["s8tf"])
            dv(lambda: V.e.tensor_tensor(out=t8b[:], in0=rr[:], in1=sth[:], op=ALU.mult), ["rr"] + S8, ["t8b"])
            dv(lambda: V.e.tensor_tensor(out=den[:], in0=are[:], in1=are[:], op=ALU.mult), P_, ["den"])
            dv(lambda: V.e.tensor_tensor(out=kr_[:], in0=aim[:], in1=aim[:], op=ALU.mult), P_, ["kr_"])
            dv(lambda: V.e.tensor_tensor(out=den[:], in0=den[:], in1=kr_[:], op=ALU.add), ["den", "kr_"], ["den"])
            dv(lambda: V.e.reciprocal(out=den[:], in_=den[:]), ["den"], ["den"])
            dv(lambda: V.e.tensor_tensor(out=kr_[:], in0=t8a[:], in1=are[:], op=ALU.mult), ["s8tf", "kr_"] + P_, ["kr_"])
            dv(lambda: V.e.tensor_tensor(out=ki_[:], in0=t8b[:], in1=aim[:], op=ALU.mult), ["t8b"] + P_, ["ki_"])
            dv(lambda: V.e.tensor_tensor(out=kr_[:], in0=kr_[:], in1=ki_[:], op=ALU.add), ["kr_", "ki_"], ["kr_"])
            dv(lambda: V.e.tensor_tensor(out=kr_[:], in0=kr_[:], in1=den[:], op=ALU.mult), ["kr_", "den"], ["kr_"])
            dv(lambda: V.e.tensor_tensor(out=ki_[:], in0=t8b[:], in1=are[:], op=ALU.mult), ["t8b", "ki_", "kr_"] + P_, ["ki_"])
            dv(lambda: V.e.tensor_tensor(out=t8b[:], in0=t8a[:], in1=aim[:], op=ALU.mult), ["s8tf", "ki_"] + P_, ["t8b"])
            dv(lambda: V.e.tensor_tensor(out=ki_[:], in0=ki_[:], in1=t8b[:], op=ALU.subtract), ["ki_", "t8b"], ["ki_"])
            dv(lambda: V.e.tensor_tensor(out=ki_[:], in0=ki_[:], in1=den[:], op=ALU.mult), ["ki_", "den"], ["ki_"])
            krb = kr_[:].unsqueeze(2).broadcast_to([128, 8, 16])
            kib = ki_[:].unsqueeze(2).broadcast_to([128, 8, 16])
            dv(lambda: V.e.tensor_tensor(out=Bbr[:], in0=Bre[:], in1=krb, op=ALU.mult), P_ + ["kr_"], ["Bbr"])
            dv(lambda: V.e.tensor_tensor(out=tB[:], in0=Bim[:], in1=kib, op=ALU.mult), P_ + ["ki_"], ["tB"])
            dv(lambda: V.e.tensor_tensor(out=Bbr[:], in0=Bbr[:], in1=tB[:], op=ALU.subtract), ["Bbr", "tB"], ["Bbr"])
            dv(lambda: V.e.tensor_tensor(out=Bbi[:], in0=Bim[:], in1=krb, op=ALU.mult), P_ + ["kr_"], ["Bbi"])
            dv(lambda: V.e.tensor_tensor(out=tB[:], in0=Bre[:], in1=kib, op=ALU.mult), P_ + ["ki_", "Bbr"], ["tB"])
            dv(lambda: V.e.tensor_tensor(out=Bbi[:], in0=Bbi[:], in1=tB[:], op=ALU.add), ["Bbi", "tB"], ["Bbi"])
            rec_split = len(rec)
            for tname, tt in (("LBr", LBr), ("LBi", LBi), ("LCr", LCr), ("LCi", LCi)):
                K.op(POOL, lambda tt=tt: POOL.e.memset(tt[:].rearrange("p a b -> p (a b)"), 0.0), W=[tname])
            for tp in range(8):
                for gl in range(2):
                    pr = slice(gl * 64, (gl + 1) * 64)
                    c0 = 32 * (tp % 4) + 16 * gl
                    K.op(POOL, lambda tp=tp, pr=pr, c0=c0: POOL.e.tensor_copy(out=LCr[pr, tp, c0:c0 + 16], in_=Cre[pr, tp, :]), R=P_ + ["LCr"], W=["LCr"])
                    K.op(POOL, lambda tp=tp, pr=pr, c0=c0: POOL.e.tensor_scalar(out=LCi[pr, tp, c0:c0 + 16], in0=Cim[pr, tp, :], scalar1=-1.0, scalar2=None, op0=ALU.mult), R=P_ + ["LCi"], W=["LCi"])
            for (src, dstL, nm) in ((Bbr, LBr, "LBr"), (Bbi, LBi, "LBi")):
                K.op(POOL, lambda: POOL.e.memset(BX[:].rearrange("p a b -> p (a b)"), 0.0), R=["BX"], W=["BX"])
                for tp in range(8):
                    for gl in range(2):
                        pr = slice(gl * 64, (gl + 1) * 64)
                        c0 = 32 * (tp % 4) + 16 * gl
                        K.op(POOL, lambda tp=tp, pr=pr, c0=c0, src=src: POOL.e.tensor_copy(out=BX[pr, tp, c0:c0 + 16], in_=src[pr, tp, :]), R=["Bbr", "Bbi", "BX"], W=["BX"])
                K.tr(R=["BX", "ident"], W=[("ps", 6)], trs=[(psT[:, 0, tp * 128:(tp + 1) * 128], BX[:, tp, :]) for tp in range(8)], ident=ident[:])
                evac(dstL[:], psT[:, 0, :].rearrange("p (a b) -> p a b", a=8), R=[("ps", 6)], W=[nm])
            K.op, K.tr, K.mm, K.dma = _real

            tail_len = len(rec) - rec_split

            def replay(n_, all_=False):
                for _ in range(n_):
                    if rec and (all_ or len(rec) > tail_len):
                        f_, a_, k_ = rec.pop(0)
                        f_(*a_, **k_)

            with ExitStack() as es_swa:
                e_sb = [es_swa.enter_context(ST("e_sb%d" % i, [128, 512], F32)) for i in range(2)]
                p_sb = [es_swa.enter_context(ST("p_sb%d" % i, [128, 512], BF16)) for i in range(4)]
                tmps = [es_swa.enter_context(ST("tmps%d" % i, [128, 512], F32)) for i in range(2)]
                items = [(n, g) for n in range(NT) for g in range(2)]
                sb_state = {"s": 0, "e": 0, "p": 0}
                qk_out = {}

                def swa_qk(idx):
                    n, g = items[idx]
                    rows = slice(g * 64, (g + 1) * 64)
                    kts = [n - 1, n] if n > 0 else [n]
                    qkeys = [("qT", j, n // 4) for j in range(4)]
                    res = []
                    for kt in kts:
                        b = sb_state["s"] % 4
                        sb_state["s"] += 1
                        K.mm(R=qkeys + [("kT", kt // 4)], W=[("ps", b)],
                             mms=[(ps[:, b, :].rearrange("p (a b) -> p a b", a=4), kT[rows, kt * 128:(kt + 1) * 128], qT[rows, :, n * 128:(n + 1) * 128], True, True)])
                        res.append((kt, b))
                    qk_out[idx] = res

                pl_out = {}

                def swa_em(idx):
                    n, g = items[idx]
                    pl = []
                    for (kt, b) in qk_out.pop(idx):
                        kind = 0 if kt == n - 1 else 1
                        ei_ = sb_state["e"] % 2
                        sb_state["e"] += 1
                        pi_ = sb_state["p"] % 4
                        sb_state["p"] += 1
                        eb = e_sb[ei_]
                        K.op(ACT, lambda eb=eb, b=b: ACT.e.activation(out=eb[:], in_=ps[:, b, :], func=AF.Exp, scale=0.125), R=[("ps", b)], W=[("e_sb", ei_)])
                        pb_ = p_sb[pi_]
                        K.op(DVE, lambda eb=eb, pb_=pb_, kind=kind: DVE.e.tensor_tensor(out=pb_[:].rearrange("p (a b) -> p a b", a=4), in0=eb[:].rearrange("p (a b) -> p a b", a=4), in1=EB[:, kind, 4 * g:4 * g + 4, :], op=ALU.mult),
                             R=[("e_sb", ei_), "EB"], W=[("p_sb", pi_)])
                        pl.append((pb_, ("p_sb", pi_), kt))
                    pl_out[idx] = pl

                def swa_fin(idx):
                    n, g = items[idx]
                    rows = slice(g * 64, (g + 1) * 64)
                    pl = pl_out.pop(idx)
                    bn = 4 + 2 * (idx % 2)
                    bs = bn + 1
                    K.mm(R=[k_ for (_, k_, _) in pl] + [("v_a", kt // 4) for (_, _, kt) in pl], W=[("ps", bn)],
                         mms=[(ps[:, bn, :], v_a[:, kt, :], pb_[:], i == 0, i == len(pl) - 1) for i, (pb_, _, kt) in enumerate(pl)])
                    K.mm(R=[k_ for (_, k_, _) in pl] + ["ones"], W=[("ps", bs)],
                         mms=[(ps[:, bs, :], ones[:], pb_[:], i == 0, i == len(pl) - 1) for i, (pb_, _, kt) in enumerate(pl)])
                    tm = tmps[g]
                    K.op(DVE, lambda tm=tm, bs=bs, rows=rows, g=g: DVE.e.tensor_tensor(out=tm[rows, :].rearrange("p (a b) -> p a b", a=4), in0=ps[rows, bs, :].rearrange("p (a b) -> p a b", a=4), in1=es_bc[rows, 4 * g:4 * g + 4].unsqueeze(2).broadcast_to([64, 4, 128]), op=ALU.add),
                         R=[("ps", bs), "es_bc"], W=[("tmps", g)])
                    K.op(ACT, lambda tm=tm, rows=rows: ACT.e.activation(out=tm[rows, :], in_=tm[rows, :], func=AF.Ln), R=[("tmps", g)], W=[("tmps", g)])
                    K.op(ACT, lambda tm=tm, rows=rows: ACT.e.activation(out=tm[rows, :], in_=tm[rows, :], func=AF.Exp, scale=-1.0), R=[("tmps", g)], W=[("tmps", g)])
                    K.op(DVE, lambda tm=tm, bn=bn, rows=rows, n=n: DVE.e.tensor_tensor(out=qT[rows, :, n * 128:(n + 1) * 128], in0=ps[rows, bn, :].rearrange("p (a b) -> p a b", a=4), in1=tm[rows, :].rearrange("p (a b) -> p a b", a=4), op=ALU.mult),
                         R=[("ps", bn), ("tmps", g)], W=[("oa", n, g)])

                NIT = len(items)
                swa_qk(0)
                swa_qk(1)
                swa_em(0)
                for idx in range(NIT):
                    if idx + 2 < NIT:
                        swa_qk(idx + 2)
                    if idx + 1 < NIT:
                        swa_em(idx + 1)
                    swa_fin(idx)
                    replay(5)
                K.barrier()
                if dbg and l == dbg.get("_layer", 0):
                    dump_bf("oaT", qT[:].rearrange("p a s -> p (a s)"), [128, 4 * S], [], es_swa)
                    K.barrier()

            replay(len(rec), all_=True)
            K.barrier()
            est.close()

            xm_re = [T_("xm_re%d" % i, [128, 4, 128]) for i in range(2)]; xm_im = [T_("xm_im%d" % i, [128, 4, 128]) for i in range(2)]
            _xt1 = T_("xt1_0", [128, 4, 128]); _xt2 = T_("xt2_0", [128, 4, 128])
            xt1 = [_xt1, _xt1]; xt2 = [_xt2, _xt2]
            q_re = [T_("q_re%d" % i, [128, 4, 128]) for i in range(2)]
            q_im = [T_("q_im%d" % i, [128, 4, 128]) for i in range(2)]
            zt1 = [T_("zt1_%d" % i, [128, 4, 128], BF16) for i in range(2)]; zt2 = [T_("zt2_%d" % i, [128, 4, 128], BF16) for i in range(2)]
            zt3 = [T_("zt3_%d" % i, [128, 4, 128], BF16) for i in range(2)]; zt4 = [T_("zt4_%d" % i, [128, 4, 128], BF16) for i in range(2)]
            ini_re = T_("ini_re", [128, 8]); ini_im = T_("ini_im", [128, 8]); it1 = T_("it1", [128, 4]); it2 = T_("it2", [128, 4])
            yc = T_("yc", [128, 2, 128]); ygc = T_("ygc", [128, 2, 128]); ygb = T_("ygb", [128, 2, 128], BF16); sgc = T_("sgc", [128, 2, 128])
            PO = POOL
            iters = [(n, hf) for n in range(NT) for hf in range(2)]

            def st0(k):
                n, hf = iters[k]
                tok = slice(n * 128, (n + 1) * 128)
                tps = list(range(4 * hf, 4 * hf + 4))
                t4 = slice(4 * hf, 4 * hf + 4)
                bre = 2 * hf
                bim = 2 * hf + 1
                K.mm(R=[("uT", hf, n // 4), "LBr"], W=[("ps", bre)],
                     mms=[(ps[:, bre, i * 128:(i + 1) * 128], LBr[:, tp, :], uT[:, hf, tok], True, True) for i, tp in enumerate(tps)])
                K.mm(R=[("uT", hf, n // 4), "LBi"], W=[("ps", bim)],
                     mms=[(ps[:, bim, i * 128:(i + 1) * 128], LBi[:, tp, :], uT[:, hf, tok], True, True) for i, tp in enumerate(tps)])

            def st0b(k):
                n, hf = iters[k]
                t4 = slice(4 * hf, 4 * hf + 4)
                bre = 2 * hf
                bim = 2 * hf + 1
                xre_v = ps[:, bre, :].rearrange("p (a b) -> p a b", a=4)
                xim_v = ps[:, bim, :].rearrange("p (a b) -> p a b", a=4)
                dv(lambda: V.e.tensor_tensor(out=xm_re[hf][:], in0=xre_v, in1=cosT[:, t4, 0:128], op=ALU.mult), [("ps", bre)] + TT, [("xm_re", hf)])
                dv(lambda: V.e.tensor_tensor(out=xt1[hf][:], in0=xim_v, in1=sinT[:, t4, 0:128], op=ALU.mult), [("ps", bim)] + TT, ["xt1"])
                dv(lambda: V.e.tensor_tensor(out=xm_im[hf][:], in0=xim_v, in1=cosT[:, t4, 0:128], op=ALU.mult), [("ps", bim)] + TT, [("xm_im", hf)])
                dv(lambda: V.e.tensor_tensor(out=xt2[hf][:], in0=xre_v, in1=nsinT[:, t4, :], op=ALU.mult), [("ps", bre), "nsinT"], ["xt2"])
                K.op(PO, lambda: PO.e.tensor_tensor(out=xm_re[hf][:], in0=xm_re[hf][:], in1=xt1[hf][:], op=ALU.add), R=[("xm_re", hf), "xt1"], W=[("xm_re", hf)])
                K.op(PO, lambda: PO.e.tensor_tensor(out=xm_im[hf][:], in0=xm_im[hf][:], in1=xt2[hf][:], op=ALU.add), R=[("xm_im", hf), "xt2"], W=[("xm_im", hf)])

            def st1(k):
                n, hf = iters[k]
                tps = list(range(4 * hf, 4 * hf + 4))
                t4 = slice(4 * hf, 4 * hf + 4)
                qr = q_re[hf]; qi = q_im[hf]
                for i, tp in enumerate(tps):
                    init_r = 0.0 if n == 0 else ini_re[:, tp:tp + 1]
                    init_i = 0.0 if n == 0 else ini_im[:, tp:tp + 1]
                    dv(lambda i=i, tp=tp, init_r=init_r: V.e.tensor_tensor_scan(out=qr[:, i, :], data0=rr[:, tp:tp + 1].broadcast_to([128, 128]), data1=xm_re[hf][:, i, :], initial=init_r, op0=ALU.mult, op1=ALU.add),
                       [("xm_re", hf), "rr", ("ini_re", hf)], [("q_re", hf)])
                    dv(lambda i=i, tp=tp, init_i=init_i: V.e.tensor_tensor_scan(out=qi[:, i, :], data0=rr[:, tp:tp + 1].broadcast_to([128, 128]), data1=xm_im[hf][:, i, :], initial=init_i, op0=ALU.mult, op1=ALU.add),
                       [("xm_im", hf), "rr", ("ini_im", hf)], [("q_im", hf)])
                QR = [("q_re", hf)]
                QI = [("q_im", hf)]
                if n < NT - 1:
                    dv(lambda: V.e.tensor_tensor(out=it1[:], in0=qr[:, :, 127], in1=cL[:, t4], op=ALU.mult), QR + CL, ["it1"])
                    dv(lambda: V.e.tensor_tensor(out=it2[:], in0=qi[:, :, 127], in1=nsL[:, t4], op=ALU.mult), QI + ["nsL"], ["it2"])
                    dv(lambda: V.e.tensor_tensor(out=ini_re[:, t4], in0=it1[:], in1=it2[:], op=ALU.add), ["it1", "it2"], [("ini_re", hf)])
                    dv(lambda: V.e.tensor_tensor(out=it1[:], in0=qr[:, :, 127], in1=sL[:, t4], op=ALU.mult), QR + CL + ["it1"], ["it1"])
                    dv(lambda: V.e.tensor_tensor(out=it2[:], in0=qi[:, :, 127], in1=cL[:, t4], op=ALU.mult), QI + CL + ["it2"], ["it2"])
                    dv(lambda: V.e.tensor_tensor(out=ini_im[:, t4], in0=it1[:], in1=it2[:], op=ALU.add), ["it1", "it2"], [("ini_im", hf)])
                z1, z2, z3, z4 = zt1[hf], zt2[hf], zt3[hf], zt4[hf]
                dv(lambda: V.e.tensor_tensor(out=z1[:], in0=qr[:], in1=cosT[:, t4, 0:128], op=ALU.mult), QR + TT, [("zt1", hf)])
                dv(lambda: V.e.tensor_tensor(out=z2[:], in0=qi[:], in1=nsinT[:, t4, :], op=ALU.mult), QI + ["nsinT"], [("zt2", hf)])
                dv(lambda: V.e.tensor_tensor(out=z3[:], in0=qr[:], in1=sinT[:, t4, 0:128], op=ALU.mult), QR + TT, [("zt3", hf)])
                dv(lambda: V.e.tensor_tensor(out=z4[:], in0=qi[:], in1=cosT[:, t4, 0:128], op=ALU.mult), QI + TT, [("zt4", hf)])
                zl = [(LCr, z1), (LCr, z2), (LCi, z3), (LCi, z4)]
                K.mm(R=[("zt1", hf), ("zt2", hf), ("zt3", hf), ("zt4", hf), "LCr", "LCi"], W=[("ps4", hf)],
                     mms=[(ps[:, 4, hf * 128:(hf + 1) * 128], LC_[:, tp, :], z_[:, i, :], (i == 0 and ri == 0), (i == 3 and ri == 3))
                          for i, tp in enumerate(tps) for ri, (LC_, z_) in enumerate(zl)])

            def st2(k):
                n, hf = iters[k]
                tok = slice(n * 128, (n + 1) * 128)
                dv(lambda: V.e.scalar_tensor_tensor(out=yc[:, hf, :], in0=uT[:, hf, tok], scalar=dcol[:, hf:hf + 1], in1=ps[:, 4, hf * 128:(hf + 1) * 128], op0=ALU.mult, op1=ALU.add),
                   [("ps4", hf), ("uT", hf, n // 4), "ssmp"], [("yc", hf)])
                if hf == 1:
                    if dbg and l == dbg.get("_layer", 0) and "yT" in dbg_d:
                        s_ = K.dsem("dbg")
                        K.dma(SP, dbg_d["yT"][:, :, tok], yc[:], R=[("yc", 0), ("yc", 1)], W=(), sem=s_)
                    K.op(ACT, lambda: ACT.e.activation(out=ygc[:], in_=yc[:], func=AF.Gelu_apprx_tanh), R=[("yc", 0), ("yc", 1)], W=["ygc"])
                    K.op(ACT, lambda: ACT.e.activation(out=ygb[:], in_=ygc[:], func=AF.Copy), R=["ygc"], W=["ygb"])
                    for h2 in range(2):
                        K.mm(R=["ygb", "wglu"], W=[("ps5", h2)], mms=[(ps[:, 5, h2 * 128:(h2 + 1) * 128], wglu[:, kk, h2 * 128:(h2 + 1) * 128], ygb[:, kk, :], kk == 0, kk == 1) for kk in range(2)])
                    K.op(ACT, lambda: ACT.e.activation(out=sgc[:], in_=ps[:, 5, 0:256].rearrange("p (a b) -> p a b", a=2), func=AF.Sigmoid), R=[("ps5", 0), ("ps5", 1)], W=["sgc"])

            def st3(n):
                tok = slice(n * 128, (n + 1) * 128)
                dv(lambda: V.e.tensor_tensor(out=uT[:, :, tok], in0=sgc[:], in1=ygc[:], op=ALU.mult), ["sgc", "ygc"], [("ob", n)])

            NI = len(iters)
            st0(0)
            st0b(0)
            st0(1)
            for k in range(NI):
                if k + 2 < NI:
                    st0(k + 2)
                if k + 1 < NI:
                    st0b(k + 1)
                st1(k)
                if k >= 1:
                    st2(k - 1)
                if k >= 2 and iters[k - 2][1] == 1:
                    st3(iters[k - 2][0])
            st2(NI - 1)
            st3(iters[NI - 1][0])
            K.barrier()
            if dbg and l == dbg.get("_layer", 0):
                dump_bf("obT", uT[:].rearrange("p a s -> p (a s)"), [128, 2 * S], [], es)
                K.barrier()

        esS.close()
        esL2 = ExitStack()
        ocT = esL2.enter_context(ST("ocT_l%d" % l, [128, 2, S], BF16))
        with ExitStack() as es:
            T_ = lambda nm, shp, dtp=F32: es.enter_context(ST(nm, shp, dtp))
            wqr = T_("wqr", [128, 2, 384], BF16)
            K.op(POOL, lambda: POOL.e.tensor_copy(out=wqr[:], in_=wq[:]), R=["wq"], W=["wqr"])
            wq4 = wq[:].rearrange("p k (h c) -> p k h c", h=4)
            wqr4 = wqr[:].rearrange("p k (h c) -> p k h c", h=4)
            for k in range(2):
                K.op(POOL, lambda k=k: POOL.e.tensor_scalar(out=wqr4[:, k, :, 64:80], in0=wq4[:, k, :, 80:96], scalar1=-1.0, scalar2=None, op0=ALU.mult), R=["wq", "wqr"], W=["wqr"])
                K.op(POOL, lambda k=k: POOL.e.tensor_copy(out=wqr4[:, k, :, 80:96], in_=wq4[:, k, :, 64:80]), R=["wq", "wqr"], W=["wqr"])
            wkv4 = wkv[:].rearrange("p (h c) -> p h c", h=4)
            R64 = slice(64, 96)
            scl = 96.0 ** -0.5
            esp = ExitStack()
            espp = ExitStack()
            Tp = lambda nm, shp, dtp=F32: esp.enter_context(ST(nm, shp, dtp))
            QT = Tp("QT", [128, 2, S], BF16)
            KT = Tp("KT", [128, 2, S], BF16)
            vm = Tp("vm", [128, NT, 128], BF16)
            eT = [Tp("eT%d" % i, [128, 512], BF16) for i in range(3)]
            rsum = [Tp("rsum%d" % i, [128, 512]) for i in range(2)]
            Tq = lambda nm, shp, dtp=F32: espp.enter_context(ST(nm, shp, dtp))
            cosRb = [Tq("cosR%d" % i, [128, 512]) for i in range(2)]; sinRb = [Tq("sinR%d" % i, [128, 512]) for i in range(2)]
            cosq = Tq("cosq", [128, 512]); sinq = Tq("sinq", [128, 512])
            sq = Tq("sq", [128, 2, 512], BF16); sqk = Tq("sqk", [128, 512], BF16)
            rq = Tq("rq", [128, 512]); rk = Tq("rk", [128, 512]); rkt = Tq("rkt", [128, 4])
            r1 = Tq("r1", [128, 512]); r2 = Tq("r2", [128, 512]); k1 = r1; k2 = r2
            kpe = Tq("kpe", [128, 512], BF16)
            for pair in range(2):
                if True:
                    if True:
                        pos_sems = [K.dsem("ropeld0"), K.dsem("ropeld1")]
                        for c in range(4):
                            tok = slice(c * 512, (c + 1) * 512)
                            cosR = cosRb[c % 2]; sinR = sinRb[c % 2]
                            ckey = ("cosR", c % 2); skey = ("sinR", c % 2)
                            K.dma(SP, sinR[R64, :], rope_d[0, :, tok], R=(), W=[skey], sem=pos_sems[c % 2])
                            K.dma(SP, cosR[R64, :], rope_d[1, :, tok], R=(), W=[ckey], sem=pos_sems[c % 2])
                            K.op(ACT, lambda tok=tok: ACT.e.activation(out=sq[:], in_=cqT[:, :, tok], func=AF.Square), R=[("cqT", 0, c), ("cqT", 1, c)], W=["sq"])
                            K.op(ACT, lambda tok=tok: ACT.e.activation(out=sqk[:], in_=ckvT[:, tok], func=AF.Square), R=[("ckvT", c)], W=["sqk"])
                            b = nbank()
                            K.mm(R=["sq", "ones"], W=[("ps", b)], mms=[(ps[:, b, :], ones[:], sq[:, k, :], k == 0, k == 1) for k in range(2)])
                            K.op(ACT, lambda b=b: ACT.e.activation(out=rq[:], in_=ps[:, b, :], func=AF.Ln, scale=1.0 / 256, bias=EPS), R=[("ps", b)], W=["rq"])
                            K.op(ACT, lambda: ACT.e.activation(out=rq[:], in_=rq[:], func=AF.Exp, scale=-0.5), R=["rq"], W=["rq"])
                            b = nbank()
                            K.mm(R=["sqk", "ones"], W=[("ps", b)], mms=[(ps[:, b, :], ones[:], sqk[:], True, True)])
                            K.op(ACT, lambda b=b: ACT.e.activation(out=rk[:], in_=ps[:, b, :], func=AF.Ln, scale=1.0 / 128, bias=EPS), R=[("ps", b)], W=["rk"])
                            K.op(ACT, lambda: ACT.e.activation(out=rk[:], in_=rk[:], func=AF.Exp, scale=-0.5), R=["rk"], W=["rk"])
                            b = nbank()
                            K.mm(R=["sqk", "ones"], W=[("ps", b)], mms=[(ps[:, b, i:i + 1], sqk[:, i * 128:(i + 1) * 128], ones[:, 0:1], True, True) for i in range(4)])
                            K.op(ACT, lambda b=b: ACT.e.activation(out=rkt[:], in_=ps[:, b, 0:4], func=AF.Ln, scale=1.0 / 128, bias=EPS), R=[("ps", b)], W=["rkt"])
                            K.op(ACT, lambda: ACT.e.activation(out=rkt[:], in_=rkt[:], func=AF.Exp, scale=-0.5), R=["rkt"], W=["rkt"])
                            K.op(POOL, lambda tok=tok, cosR=cosR: POOL.e.tensor_tensor(out=k2[R64, :], in0=krT[R64, tok], in1=cosR[R64, :], op=ALU.mult), R=[("krT", c), ckey], W=["r2"])
                            K.op(POOL, lambda tok=tok, sinR=sinR: POOL.e.tensor_tensor(out=k1[R64, :], in0=krotT[R64, tok], in1=sinR[R64, :], op=ALU.mult), R=[("krotT", c), skey], W=["r1"])
                            K.op(POOL, lambda: POOL.e.tensor_tensor(out=kpe[R64, :], in0=k2[R64, :], in1=k1[R64, :], op=ALU.add), R=["r1", "r2"], W=["kpe"])
                            K.op(DVE, lambda cosR=cosR: DVE.e.tensor_tensor(out=cosq[R64, :], in0=cosR[R64, :], in1=rq[R64, :], op=ALU.mult), R=[ckey, "rq"], W=["cosq"])
                            K.op(DVE, lambda sinR=sinR: DVE.e.tensor_tensor(out=sinq[R64, :], in0=sinR[R64, :], in1=rq[R64, :], op=ALU.mult), R=[skey, "rq"], W=["sinq"])
                            for hl in range(2):
                                h = 2 * pair + hl
                                ba = nbank()
                                K.mm(R=["wq", ("cqT", 0, c), ("cqT", 1, c)], W=[("ps", ba)], mms=[(ps[0:96, ba, :], wq[:, k, h * 96:(h + 1) * 96], cqT[:, k, tok], k == 0, k == 1) for k in range(2)])
                                bb = nbank()
                                K.mm(R=["wqr", ("cqT", 0, c), ("cqT", 1, c)], W=[("ps", bb)], mms=[(ps[0:96, bb, :], wqr[:, k, h * 96:(h + 1) * 96], cqT[:, k, tok], k == 0, k == 1) for k in range(2)])
                                K.op(DVE, lambda hl=hl, ba=ba, tok=tok: DVE.e.tensor_tensor(out=QT[0:64, hl, tok], in0=ps[0:64, ba, :], in1=rq[0:64, :], op=ALU.mult), R=[("ps", ba), "rq"], W=[("QT", hl, c)])
                                K.op(DVE, lambda ba=ba: DVE.e.tensor_tensor(out=r1[R64, :], in0=ps[R64, ba, :], in1=cosq[R64, :], op=ALU.mult), R=[("ps", ba), "cosq"], W=["r1"])
                                K.op(DVE, lambda bb=bb: DVE.e.tensor_tensor(out=r2[R64, :], in0=ps[R64, bb, :], in1=sinq[R64, :], op=ALU.mult), R=[("ps", bb), "sinq"], W=["r2"])
                                K.op(DVE, lambda hl=hl, tok=tok: DVE.e.tensor_tensor(out=QT[R64, hl, tok], in0=r1[R64, :], in1=r2[R64, :], op=ALU.add), R=["r1", "r2"], W=[("QTp", hl, c)])
                                bk_ = nbank()
                                K.mm(R=["wkv", ("ckvT", c)], W=[("ps", bk_)], mms=[(ps[0:64, bk_, :], wkv[:, h * 128:h * 128 + 64], ckvT[:, tok], True, True)])
                                K.op(DVE, lambda hl=hl, bk_=bk_, tok=tok: DVE.e.tensor_tensor(out=KT[0:64, hl, tok], in0=ps[0:64, bk_, :], in1=rk[0:64, :], op=ALU.mult), R=[("ps", bk_), "rk"], W=[("KT", hl, c)])
                                K.op(ACT, lambda hl=hl, tok=tok: ACT.e.activation(out=KT[R64, hl, tok], in_=kpe[R64, :], func=AF.Copy), R=["kpe"], W=[("KTp", hl, c)])
                            bv = nbank()
                            K.mm(R=["wkv", ("ckvT", c)], W=[("ps", bv)],
                                 mms=[(ps[:, bv, i * 128:(i + 1) * 128].rearrange("p (h d) -> p h d", h=2), ckvT[:, c * 512 + i * 128:c * 512 + (i + 1) * 128], wkv4[:, 2 * pair:2 * pair + 2, 64:128], True, True) for i in range(4)])
                            K.op(DVE, lambda bv=bv, c=c: DVE.e.tensor_tensor(out=vm[:, 4 * c:4 * c + 4, :], in0=ps[:, bv, :].rearrange("p (a b) -> p a b", a=4), in1=rkt[:, 0:4].unsqueeze(2).broadcast_to([128, 4, 128]), op=ALU.mult), R=[("ps", bv), "rkt"], W=[("vm", 4 * c + i) for i in range(4)])
                        if dbg and l == dbg.get("_layer", 0):
                            K.barrier()
                            dump_bf("QT%d" % pair, QT[0:96, :, :].rearrange("p a s -> p (a s)"), [96, 2 * S], [], espp)
                            dump_bf("KT%d" % pair, KT[0:96, :, :].rearrange("p a s -> p (a s)"), [96, 2 * S], [], espp)
                            dump_bf("vm%d" % pair, vm[:].rearrange("p a s -> p (a s)"), [128, NT * 128], [], espp)
                            K.barrier()
                    with ExitStack() as esa:
                        aitems = [(hl, Qc, j) for hl in range(2) for Qc in range(4) for j in range(4 * Qc + 4)]
                        ast = {"s": 0, "e": 0, "f": 0}
                        aqk = {}

                        def geom(Qc, j):
                            qb0 = max(j, 4 * Qc)
                            c0 = (qb0 - 4 * Qc) * 128
                            return qb0, c0, 512 - c0

                        def mla_qk(idx):
                            hl, Qc, j = aitems[idx]
                            qb0, c0, ncol = geom(Qc, j)
                            qtok = slice(qb0 * 128, (4 * Qc + 4) * 128)
                            b = ast["s"] % 2
                            ast["s"] += 1
                            K.mm(R=[("QT", hl, Qc), ("QTp", hl, Qc), ("KT", hl, j // 4), ("KTp", hl, j // 4)], W=[("ps", b)],
                                 mms=[(ps[:, b, 0:ncol], KT[0:96, hl, j * 128:(j + 1) * 128], QT[0:96, hl, qtok], True, True)])
                            aqk[idx] = b

                        def mla_rest(idx):
                            hl, Qc, j = aitems[idx]
                            qb0, c0, ncol = geom(Qc, j)
                            nj = 4 * Qc + 4
                            grp = hl * 4 + Qc
                            bn = 2 + 2 * (grp % 2)
                            bs = bn + 1
                            hr = slice(hl * 64, hl * 64 + 64)
                            b = aqk.pop(idx)
                            ei_ = ast["e"] % 3
                            ast["e"] += 1
                            et = eT[ei_]
                            K.op(ACT, lambda et=et, b=b, ncol=ncol: ACT.e.activation(out=et[:, 0:ncol], in_=ps[:, b, 0:ncol], func=AF.Exp, scale=scl), R=[("ps", b)], W=[("eT", ei_)])
                            if j >= 4 * Qc:
                                K.op(POOL, lambda et=et: POOL.e.affine_select(out=et[:, 0:128], in_=et[:, 0:128], pattern=[[1, 128]], compare_op=ALU.is_ge, fill=0.0, base=0, channel_multiplier=-1), R=[("eT", ei_)], W=[("eT", ei_)])
                            K.mm(R=[("eT", ei_), ("vm", j)], W=[("ps", bn)], mms=[(ps[:, bn, c0:512], vm[:, j, :], et[:, 0:ncol], j == 0, j == nj - 1)])
                            K.mm(R=[("eT", ei_), "ones"], W=[("ps", bs)], mms=[(ps[:, bs, c0:512], ones[:], et[:, 0:ncol], j == 0, j == nj - 1)])
                            if j == nj - 1:
                                fi_ = ast["f"] % 2
                                ast["f"] += 1
                                rs = rsum[fi_]
                                K.op(DVE, lambda rs=rs, bs=bs, hr=hr: DVE.e.reciprocal(out=rs[hr, :], in_=ps[hr, bs, :]), R=[("ps", bs)], W=[("rsum", fi_)])
                                K.op(DVE, lambda rs=rs, bn=bn, hr=hr, Qc=Qc: DVE.e.tensor_tensor(out=ocT[hr, pair, Qc * 512:(Qc + 1) * 512], in0=ps[hr, bn, :], in1=rs[hr, :], op=ALU.mult), R=[("ps", bn), ("rsum", fi_)], W=[("ocT", pair, hl, Qc)])

                        mla_qk(0)
                        for idx in range(len(aitems)):
                            if idx + 1 < len(aitems):
                                mla_qk(idx + 1)
                            mla_rest(idx)
                        if pair == 1:
                            K.barrier()
                            espp.close()
                            esp.close()
            if dbg and l == dbg.get("_layer", 0):
                dump_bf("ocT", ocT[:].rearrange("p a s -> p (a s)"), [128, 2 * S], [], es)
                K.barrier()

        def mixchunk(k, tsl):
            if k < 4:
                return qT[:, k, tsl]
            if k < 6:
                return uT[:, k - 4, tsl]
            return ocT[:, k - 6, tsl]

        sqj = esL2.enter_context(ST("sqj", [128, D], BF16))
        pre_gu = {}
        g2p = gcol[:, 2 * l + 1, :].unsqueeze(2).broadcast_to([128, 8, 128])
        for f_ in range(2):
            sgp = 8 + (ring_alloc() % 4)
            sup = 8 + (ring_alloc() % 4)
            for (sl, wsrc) in ((sgp, wg_d), (sup, wu_d)):
                rv = ring[sl][:].rearrange("p (k c) -> p k c", k=8)
                wload(wsrc[l].rearrange("(k p) c -> p k c", p=128)[:, :, f_ * 128:(f_ + 1) * 128], sl, rv)
            for sl in (sgp, sup):
                rv = ring[sl][:].rearrange("p (k c) -> p k c", k=8)
                K.op(POOL, lambda rv=rv: POOL.e.tensor_tensor(out=rv, in0=rv, in1=g2p, op=ALU.mult), R=[("ring", sl), "gcol"], W=[("ring", sl)])
            pre_gu[f_] = (sgp, sup)
        for t in range(NT):
            tsl = slice(t * 128, (t + 1) * 128)
            for hf in range(2):
                b = nbank()
                K.mm(R=[("ring", s_) for s_ in wo_units], W=[("ps", b)], mms=[(ps[:, b, :], mixchunk(k, tsl), ring[wo_units[k]][:, hf * 512:(hf + 1) * 512], k == 0, k == 7) for k in range(8)])
                K.op(DVE, lambda t=t, hf=hf, b=b: DVE.e.tensor_tensor(out=x_sb[:, t, hf * 512:(hf + 1) * 512], in0=x_sb[:, t, hf * 512:(hf + 1) * 512], in1=ps[:, b, :], op=ALU.add), R=[("ps", b), ("x", t)], W=[("x", t)])
            K.op(ACT, lambda t=t: ACT.e.activation(out=sqj[:], in_=x_sb[:, t, :], func=AF.Square, accum_out=ss[:, t:t + 1]), R=[("x", t)], W=["sqj", ("ss", t)])
        rstd_from_ss(ss[:, 0:NT], rstd[:, 0:NT], NT, D, [("ss", t) for t in range(NT)], [("rstd", t) for t in range(NT)], None)
        K.barrier(keep=lambda k: isinstance(k, tuple) and k[0] == "rstd")
        esL2.close()
        esL.close()
        if dbg and l == dbg.get("_layer", 0):
            dbg_dump("x_mid", x_sb[:].rearrange("p t d -> p (t d)"), [])
            K.barrier()

        with ExitStack() as es:
            T_ = lambda nm, shp, dtp=F32: es.enter_context(ST(nm, shp, dtp))
            h2T = T_("h2T", [128, 8, S], BF16)
            actT = T_("actT", [128, 8, S], BF16)
            junk = T_("junk2", [128, D], BF16)
            hb = [T_("hb2_%d" % i, [128, D], BF16) for i in range(2)]
            sl_ = [T_("silu%d" % i, [128, 512], BF16) for i in range(2)]
            def norm2_chunk(c):
                norm_tiles(list(range(4 * c, 4 * c + 4)), lambda i, c=c: h2T[:, :, (4 * c + i) * 128:(4 * c + i + 1) * 128], lambda i, c=c: ("h2T", c, i), junk, hb, stats=False)

            norm2_chunk(0)
            g2 = gcol[:, 2 * l + 1, :].unsqueeze(2).broadcast_to([128, 8, 128])
            wgv = wg_d[l].rearrange("(k p) c -> p k c", p=128)
            wuv = wu_d[l].rearrange("(k p) c -> p k c", p=128)
            thirds = [(0, 8), (8, 15), (15, 22)]
            si = 0
            for (f0, f1) in thirds:
                wd_units = [f - f0 for f in range(f0, f1)]
                for f in range(f0, f1):
                    if f in pre_gu:
                        sg_, su_ = pre_gu[f]
                    else:
                        sg_ = 8 + (ring_alloc() % 4)
                        su_ = 8 + (ring_alloc() % 4)
                        for (sl, wv_) in ((sg_, wgv), (su_, wuv)):
                            rv = ring[sl][:].rearrange("p (k c) -> p k c", k=8)
                            wload(wv_[:, :, f * 128:(f + 1) * 128], sl, rv)
                        for (sl, wv_) in ((sg_, wgv), (su_, wuv)):
                            rv = ring[sl][:].rearrange("p (k c) -> p k c", k=8)
                            K.op(POOL, lambda rv=rv: POOL.e.tensor_tensor(out=rv, in0=rv, in1=g2, op=ALU.mult), R=[("ring", sl), "gcol"], W=[("ring", sl)])
                    if f == f0 + 1:
                        for f_ in range(f0, f1):
                            wload(wd_d[l, f_ * 128:(f_ + 1) * 128, :], f_ - f0)
                    rg = ring[sg_][:].rearrange("p (k c) -> p k c", k=8)
                    ru = ring[su_][:].rearrange("p (k c) -> p k c", k=8)
                    for c in range(4):
                        tok = slice(c * 512, (c + 1) * 512)
                        if f == 0 and c + 1 < 4:
                            norm2_chunk(c + 1)
                        bg = nbank()
                        K.mm(R=[("ring", sg_)] + [("h2T", c, i_) for i_ in range(4)], W=[("ps", bg)], mms=[(ps[:, bg, :], rg[:, k, :], h2T[:, k, tok], k == 0, k == 7) for k in range(8)])
                        bu = nbank()
                        K.mm(R=[("ring", su_)] + [("h2T", c, i_) for i_ in range(4)], W=[("ps", bu)], mms=[(ps[:, bu, :], ru[:, k, :], h2T[:, k, tok], k == 0, k == 7) for k in range(8)])
                        sb_ = sl_[si % 2]
                        K.op(ACT, lambda sb_=sb_, bg=bg: ACT.e.activation(out=sb_[:], in_=ps[:, bg, :], func=AF.Silu), R=[("ps", bg)], W=[("silu", si % 2)])
                        K.op(DVE, lambda sb_=sb_, bu=bu, f=f, f0=f0, tok=tok: DVE.e.tensor_tensor(out=actT[:, f - f0, tok], in0=sb_[:], in1=ps[:, bu, :], op=ALU.mult), R=[("silu", si % 2), ("ps", bu)], W=[("actT", f - f0, c)])
                        si += 1
                nf = f1 - f0
                for t in range(NT):
                    tsl = slice(t * 128, (t + 1) * 128)
                    for hf in range(2):
                        b = nbank()
                        K.mm(R=[("ring", s_) for s_ in wd_units] + [("actT", i, t // 4) for i in range(nf)], W=[("ps", b)],
                             mms=[(ps[:, b, :], actT[:, i, tsl], ring[wd_units[i]][:, hf * 512:(hf + 1) * 512], i == 0, i == nf - 1) for i in range(nf)])
                        K.op(DVE, lambda t=t, hf=hf, b=b: DVE.e.tensor_tensor(out=x_sb[:, t, hf * 512:(hf + 1) * 512], in0=x_sb[:, t, hf * 512:(hf + 1) * 512], in1=ps[:, b, :], op=ALU.add), R=[("ps", b), ("x", t)], W=[("x", t)])
                        if hf == 1 and l == nlayers - 1 and f1 == NF:
                            K.op(ACT, lambda t=t: ACT.e.activation(out=junk[:], in_=x_sb[:, t, :], func=AF.Square, accum_out=ss[:, t:t + 1]), R=[("x", t)], W=["junk", ("ss", t)])
            if l == nlayers - 1:
                rstd_from_ss(ss[:, 0:NT], rstd[:, 0:NT], NT, D, [("ss", t) for t in range(NT)], [("rstd", t) for t in range(NT)], None)
            snapF = K.snapshot()
            if l + 1 < nlayers:
                load_w_in(l + 1)
            K.barrier(snapF, keep=isring)
        if dbg and l == dbg.get("_layer", 0):
            dbg_dump("x_out", x_sb[:].rearrange("p t d -> p (t d)"), [])
            K.barrier()

    with ExitStack() as es:
        fg = es.enter_context(ST("fg", [128, D], F32))
        junk = es.enter_context(ST("junk3", [128, D], BF16))
        ob = [es.enter_context(ST("ob%d" % i, [128, D], F32)) for i in range(2)]
        fsem = K.dsem("fg")
        osem = K.dsem("out")
        K.dma(SP, fg[:], fg_d.partition_broadcast(128), R=(), W=["fg"], sem=fsem)
        orr = out_d.rearrange("(t p) d -> p t d", p=128)
        for t in range(NT):
            o_ = ob[t % 2]
            K.op(DVE, lambda t=t, o_=o_: DVE.e.scalar_tensor_tensor(out=o_[:], in0=x_sb[:, t, :], scalar=rstd[:, t:t + 1], in1=fg[:], op0=ALU.mult, op1=ALU.mult), R=[("x", t), ("rstd", t), "fg"], W=[("ob", t % 2)])
            K.dma(SP, orr[:, t, :], o_[:], R=[("ob", t % 2)], W=(), sem=osem)
        K.barrier()
    es0.close()
    return nc


_CACHE = {}


def kernel(**inputs):
    if "nc" not in _CACHE:
        _CACHE["nc"] = build()
    nc = _CACHE["nc"]
    B = inputs["x"].shape[0]
    shared = {k: np.ascontiguousarray(np.asarray(v)) for k, v in inputs.items() if k not in ("x", "positions")}
    x = np.asarray(inputs["x"])
    pos = np.asarray(inputs["positions"]).astype(np.int32)
    in_maps = []
    for b in range(B):
        m = dict(shared)
        m["x"] = np.ascontiguousarray(x[b])
        m["positions"] = np.ascontiguousarray(pos[b])
        in_maps.append(m)
    res = run_bass_kernel_spmd(nc, in_maps, core_ids=list(range(B)))
    return np.stack([np.asarray(r["out"]) for r in res.results], axis=0).astype(np.float32)
```

```python
import math
from contextlib import ExitStack

import numpy as np
import concourse.bass as bass
import concourse.mybir as mybir
from concourse.bass_utils import run_bass_kernel_spmd

F32 = mybir.dt.float32
BF16 = mybir.dt.bfloat16
I32 = mybir.dt.int32
AF = mybir.ActivationFunctionType
ALU = mybir.AluOpType

S = 2048
D = 1024
NT = 16
DFF = 2816
NF = 22
EPS = 1e-6
TWO_PI = 2.0 * math.pi
PI_LO = 3.1415925


class Sem:
    def __init__(self, nc, name):
        self.h = nc.alloc_semaphore(name)
        self.count = 0


class Eng:
    def __init__(self, nc, name, e):
        self.name = name
        self.e = e
        self.sem = Sem(nc, "s_" + name)
        self.seen = {}


class KB:
    def __init__(self, nc):
        self.nc = nc
        self.PE = Eng(nc, "pe", nc.tensor)
        self.ACT = Eng(nc, "act", nc.scalar)
        self.DVE = Eng(nc, "dve", nc.vector)
        self.POOL = Eng(nc, "pool", nc.gpsimd)
        self.SP = Eng(nc, "sp", nc.sync)
        self.engs = [self.PE, self.ACT, self.DVE, self.POOL, self.SP]
        self.lw = {}
        self.rd = {}
        self.dsems = []
        self.nsem = 0

    def dsem(self, name):
        s = Sem(self.nc, "d_%s_%d" % (name, self.nsem))
        self.nsem += 1
        self.dsems.append(s)
        return s

    def _wait(self, E, need):
        for s, v in need.items():
            if E.seen.get(s, 0) < v:
                E.e.wait_ge(s.h, v)
                E.seen[s] = v

    def _need(self, E, R, W):
        need = {}
        for k in R:
            lw = self.lw.get(k)
            if lw is not None:
                need[lw[0]] = max(need.get(lw[0], 0), lw[1])
        for k in W:
            lw = self.lw.get(k)
            if lw is not None and lw[0] is not E.sem:
                need[lw[0]] = max(need.get(lw[0], 0), lw[1])
            for s, v in self.rd.get(k, {}).items():
                if s is not E.sem:
                    need[s] = max(need.get(s, 0), v)
        return need

    def _record(self, sem, val, R, W):
        for k in W:
            self.lw[k] = (sem, val)
            self.rd[k] = {}
        for k in R:
            d = self.rd.setdefault(k, {})
            d[sem] = max(d.get(sem, 0), val)

    def op(self, E, fn, R=(), W=(), sig=True):
        self._wait(E, self._need(E, R, W))
        ins = fn()
        if sig:
            E.sem.count += 1
            ins.then_inc(E.sem.h, 1)
            val = E.sem.count
        else:
            val = E.sem.count + 1
        self._record(E.sem, val, R, W)
        return ins

    def mm(self, R, W, mms):
        E = self.PE
        self._wait(E, self._need(E, R, W))
        n = len(mms)
        for i, (o, l, r, st, sp) in enumerate(mms):
            ins = E.e.matmul(o, lhsT=l, rhs=r, start=st, stop=sp)
            if i == n - 1:
                E.sem.count += 1
                ins.then_inc(E.sem.h, 1)
        self._record(E.sem, E.sem.count, R, W)

    def tr(self, R, W, trs, ident):
        E = self.PE
        self._wait(E, self._need(E, R, W))
        n = len(trs)
        for i, (o, a) in enumerate(trs):
            ins = E.e.transpose(out=o, in_=a, identity=ident)
            if i == n - 1:
                E.sem.count += 1
                ins.then_inc(E.sem.h, 1)
        self._record(E.sem, E.sem.count, R, W)

    def dma(self, Q, out, in_, R, W, sem):
        self._wait(Q, self._need(Q, R, W))
        ins = Q.e.dma_start(out=out, in_=in_)
        sem.count += 16
        ins.then_inc(sem.h, 16)
        self._record(sem, sem.count, R, W)

    def snapshot(self):
        snap = {}
        for F in self.engs:
            if F.sem.count > 0:
                snap[F.sem] = F.sem.count
        for s_ in self.dsems:
            if s_.count > 0:
                snap[s_] = s_.count
        return snap

    def barrier(self, snap=None, keep=None):
        if snap is None:
            snap = self.snapshot()
        for E in self.engs:
            self._wait(E, {s_: v for s_, v in snap.items() if s_ is not E.sem})
        if keep is None:
            self.lw = {}
            self.rd = {}
        else:
            self.lw = {k: v for k, v in self.lw.items() if keep(k)}
            self.rd = {k: v for k, v in self.rd.items() if keep(k)}


def t5_bucket_np():
    d = np.arange(128)
    n = np.maximum(d, 1).astype(np.float32)
    large = 16 + (np.log(n / np.float32(16)) / np.float32(math.log(128 / 16)) * np.float32(16)).astype(np.int32)
    large = np.minimum(large, 31)
    return np.where(d < 16, d, large)


def build(nlayers=2, dbg=None):
    nc = bass.Bass("TRN2", target_bir_lowering=False)
    dt = nc.dram_tensor
    x_d = dt("x", [S, D], F32, kind="ExternalInput").ap()
    pos_d = dt("positions", [S], I32, kind="ExternalInput").ap()
    relb_d = dt("rel_bias", [32, 8], F32, kind="ExternalInput").ap()
    ln1_d = dt("ln1_g", [2, D], F32, kind="ExternalInput").ap()
    win_d = dt("w_in", [2, D, 1440], F32, kind="ExternalInput").ap()
    sinks_d = dt("sinks", [2, 8], F32, kind="ExternalInput").ap()
    are_d = dt("ssm_a_re", [2, 16, 64], F32, kind="ExternalInput").ap()
    aim_d = dt("ssm_a_im", [2, 16, 64], F32, kind="ExternalInput").ap()
    ldt_d = dt("ssm_log_dt", [2, 16], F32, kind="ExternalInput").ap()
    bre_d = dt("ssm_b_re", [2, 16, 64, 16], F32, kind="ExternalInput").ap()
    bim_d = dt("ssm_b_im", [2, 16, 64, 16], F32, kind="ExternalInput").ap()
    cre_d = dt("ssm_c_re", [2, 16, 16, 64], F32, kind="ExternalInput").ap()
    cim_d = dt("ssm_c_im", [2, 16, 16, 64], F32, kind="ExternalInput").ap()
    sd_d = dt("ssm_d", [2, 256], F32, kind="ExternalInput").ap()
    wglu_d = dt("ssm_w_glu", [2, 256, 256], F32, kind="ExternalInput").ap()
    gq_d = dt("mla_q_norm_g", [2, 256], F32, kind="ExternalInput").ap()
    wq_d = dt("mla_w_q_up", [2, 256, 384], F32, kind="ExternalInput").ap()
    gkv_d = dt("mla_kv_norm_g", [2, 128], F32, kind="ExternalInput").ap()
    wkv_d = dt("mla_w_kv_up", [2, 128, 512], F32, kind="ExternalInput").ap()
    wout_d = dt("w_out", [2, D, D], F32, kind="ExternalInput").ap()
    ln2_d = dt("ln2_g", [2, D], F32, kind="ExternalInput").ap()
    wg_d = dt("w_gate", [2, D, DFF], F32, kind="ExternalInput").ap()
    wu_d = dt("w_up", [2, D, DFF], F32, kind="ExternalInput").ap()
    wd_d = dt("w_down", [2, DFF, D], F32, kind="ExternalInput").ap()
    fg_d = dt("final_g", [D], F32, kind="ExternalInput").ap()
    out_d = dt("out", [S, D], F32, kind="ExternalOutput").ap()
    gd_d = dt("gd_scratch", [2, 8, 128, 256], F32, kind="Internal").ap()
    rope_d = dt("rope_scratch", [2, 32, S], F32, kind="Internal").ap()
    dbg_d = {}
    if dbg:
        for nm, shp in dbg.items():
            if nm.startswith("_"):
                continue
            dbg_d[nm] = dt("dbg_" + nm, shp, F32, kind="ExternalOutput").ap()

    K = KB(nc)
    PE, ACT, DVE, POOL, SP = K.PE, K.ACT, K.DVE, K.POOL, K.SP
    _cnt = [0]

    def ST(nm, shp, dtp):
        _cnt[0] += 1
        return nc.sbuf_tensor("%s_u%d" % (nm, _cnt[0]), shp, dtp)

    A = nc.alloc_sbuf_tensor

    es0 = ExitStack()
    ps = es0.enter_context(nc.psum_tensor("ps", [128, 8, 512], F32))
    psT = ps[:, 6:8, :].bitcast(BF16)
    nc_ctx = es0.enter_context(nc.allow_non_contiguous_dma(reason="small param layouts"))

    x_sb = A("x_sb", [128, NT, D], F32)
    NRING = 12
    ring = [A("ring%d" % i, [128, 1024], BF16) for i in range(NRING)]
    rsem = [K.dsem("ring") for _ in range(NRING)]
    ident = A("ident", [128, 128], BF16)
    ones = A("ones", [128, 128], BF16)
    iota_i = A("iota_i", [128, 129], F32)
    pidx = A("pidx", [128, 1], F32)
    invf = A("invf", [128, 1], F32)
    EB = A("EB", [128, 2, 8, 128], F32)
    ss = A("ss", [128, NT], F32)
    rstd = A("rstd", [128, NT], F32)
    gcol = A("gcol", [128, 4, 8], F32)
    ring_state = {"next": 0}

    def ring_alloc():
        i = ring_state["next"] % NRING
        ring_state["next"] += 1
        return i

    def wload(src_ap, slot, view=None):
        o = ring[slot][:] if view is None else view
        K.dma(POOL, o, src_ap, R=(), W=[("ring", slot)], sem=rsem[slot])

    psem = K.dsem("params")
    for l in range(2):
        K.dma(SP, gcol[:, 2 * l, :], ln1_d[l].rearrange("(k p) -> p k", p=128), R=(), W=["gcol"], sem=psem)
        K.dma(SP, gcol[:, 2 * l + 1, :], ln2_d[l].rearrange("(k p) -> p k", p=128), R=(), W=["gcol"], sem=psem)
    es_setup = ExitStack()
    rb = es_setup.enter_context(ST("rb", [128, 32, 8], F32))
    rposi = es_setup.enter_context(ST("rposi", [128, 4, 512], I32))
    K.dma(SP, rb[:].rearrange("p a b -> p (a b)"), relb_d.rearrange("a b -> (a b)").partition_broadcast(128), R=(), W=["rb"], sem=K.dsem("rb"))
    K.dma(SP, rposi[64:96, :, :].rearrange("p a b -> p (a b)"), pos_d.partition_broadcast(32), R=(), W=["rposi"], sem=K.dsem("rpos"))
    xr = x_d.rearrange("(t p) d -> p t d", p=128)
    for c in range(4):
        K.dma(SP, x_sb[:, 4 * c:4 * c + 4, :], xr[:, 4 * c:4 * c + 4, :], R=(), W=[("x", t) for t in range(4 * c, 4 * c + 4)], sem=K.dsem("x%d" % c))

    def load_w_in(l, scale=True, dma=True, seng=None):
        wv = win_d[l].rearrange("(k p) c -> p k c", p=128)
        g1 = gcol[:, 2 * l, :].unsqueeze(2).broadcast_to([128, 8, 128])
        order = [7, 8, 9, 10, 11, 0, 1, 2, 3, 4, 5, 6]
        if dma:
            for j in order:
                sl = j
                rv = ring[sl][:].rearrange("p (k c) -> p k c", k=8)
                if j < 4:
                    wload(wv[:, :, j * 64:(j + 1) * 64], sl, rv[:, :, 0:64])
                    wload(wv[:, :, (4 + j) * 64:(5 + j) * 64], sl, rv[:, :, 64:128])
                elif j < 11:
                    c0 = 512 + (j - 4) * 128
                    wload(wv[:, :, c0:c0 + 128], sl, rv)
                else:
                    wload(wv[:, :, 1344:1408], sl, rv[:, :, 0:64])
                    wload(wv[:, :, 1408:1440], sl, rv[:, :, 64:96])
                    wload(wv[:, :, 1424:1440], sl, rv[:, :, 96:112])
                    wload(wv[:, :, 1408:1424], sl, rv[:, :, 112:128])
        if scale:
            for j in order:
                sl = j
                rv = ring[sl][:].rearrange("p (k c) -> p k c", k=8)
                SE = seng if seng is not None else POOL
                if j == 11:
                    K.op(SE, lambda rv=rv: SE.e.tensor_scalar(out=rv[:, :, 96:112], in0=rv[:, :, 96:112], scalar1=-1.0, scalar2=None, op0=ALU.mult), R=[("ring", sl)], W=[("ring", sl)])
                K.op(SE, lambda rv=rv: SE.e.tensor_tensor(out=rv, in0=rv, in1=g1, op=ALU.mult), R=[("ring", sl), "gcol"], W=[("ring", sl)])

    isring = lambda k: isinstance(k, tuple) and k[0] == "ring"
    with es_setup as es:
        identf = es.enter_context(ST("identf", [128, 128], F32))
        G = es.enter_context(ST("G", [128, 2, 8, 256], F32))
        K.op(POOL, lambda: POOL.e.memset(identf[:], 0.0), W=["identf"])
        K.op(POOL, lambda: POOL.e.affine_select(out=identf[:], in_=identf[:], pattern=[[-1, 128]], compare_op=ALU.not_equal, fill=1.0, base=0, channel_multiplier=1), R=["identf"], W=["identf"])
        K.op(POOL, lambda: POOL.e.iota(iota_i[:], pattern=[[1, 129]], base=0, channel_multiplier=0, allow_small_or_imprecise_dtypes=True), W=["iota_i"])
        K.op(POOL, lambda: POOL.e.iota(pidx[:], pattern=[[0, 1]], base=0, channel_multiplier=1, allow_small_or_imprecise_dtypes=True), W=["pidx"])
        K.op(POOL, lambda: POOL.e.memset(G[:].rearrange("p a b c -> p (a b c)"), 0.0), W=["G"])
        wstg = es.enter_context(ST("wstg", [128, 8, 1440], F32))
        wsem = K.dsem("wstg")
        wv0 = win_d[0].rearrange("(k p) c -> p k c", p=128)
        for k2 in range(4):
            K.dma(SP, wstg[:, 2 * k2:2 * k2 + 2, :], wv0[:, 2 * k2:2 * k2 + 2, :], R=(), W=["wstg"], sem=wsem)
        K.op(DVE, lambda: DVE.e.tensor_copy(out=ident[:], in_=identf[:]), R=["identf"], W=["ident"])
        K.op(DVE, lambda: DVE.e.memset(ones[:], 1.0), W=["ones"])
        K.op(DVE, lambda: DVE.e.tensor_scalar(out=invf[:], in0=pidx[:], scalar1=1.0 / 16.0, scalar2=None, op0=ALU.mult), R=["pidx"], W=["invf"])
        kki = es.enter_context(ST("kki", [128, 1], I32))
        kkf = es.enter_context(ST("kkf", [128, 1], F32))
        K.op(DVE, lambda: DVE.e.tensor_scalar(out=kkf[:], in0=invf[:], scalar1=-0.46875, scalar2=None, op0=ALU.add), R=["invf"], W=["kkf"])
        K.op(DVE, lambda: DVE.e.tensor_copy(out=kki[:], in_=kkf[:]), R=["kkf"], W=["kki"])
        K.op(DVE, lambda: DVE.e.tensor_copy(out=kkf[:], in_=kki[:]), R=["kki"], W=["kkf"])
        K.op(DVE, lambda: DVE.e.scalar_tensor_tensor(out=invf[:], in0=kkf[:], scalar=-16.0, in1=pidx[:], op0=ALU.mult, op1=ALU.add), R=["kkf", "pidx"], W=["invf"])
        K.op(ACT, lambda: ACT.e.activation(out=invf[:], in_=invf[:], func=AF.Exp, scale=-math.log(10000.0) / 16.0), R=["invf"], W=["invf"])
        bk = t5_bucket_np()
        runs = []
        d0 = 0
        for d in range(1, 129):
            if d == 128 or bk[d] != bk[d0]:
                runs.append((d0, d, int(bk[d0])))
                d0 = d
        gk = []
        for (lo, hi, b) in runs:
            src = rb[:, b, :].unsqueeze(2).broadcast_to([128, 8, hi - lo])
            lo_p = 1 if lo == 0 else lo
            if hi > lo_p:
                srcp = rb[:, b, :].unsqueeze(2).broadcast_to([128, 8, hi - lo_p])
                key = ("G", len(gk)); gk.append(key)
                K.op(DVE, lambda srcp=srcp, lo_p=lo_p, hi=hi: DVE.e.tensor_copy(out=G[:, 0, :, lo_p:hi], in_=srcp), R=["rb", "G"], W=[key])
            key = ("G", len(gk)); gk.append(key)
            K.op(DVE, lambda src=src, lo=lo, hi=hi: DVE.e.tensor_copy(out=G[:, 1, :, 128 + lo:128 + hi], in_=src), R=["rb", "G"], W=[key])
        K.op(ACT, lambda: ACT.e.activation(out=G[:, 0, :, 1:128], in_=G[:, 0, :, 1:128], func=AF.Exp), R=["G"] + gk, W=["G"])
        K.op(ACT, lambda: ACT.e.activation(out=G[:, 1, :, 128:256], in_=G[:, 1, :, 128:256], func=AF.Exp), R=["G"] + gk, W=["G"])
        R64s = slice(64, 96)
        rposf = es.enter_context(ST("rposf", [128, 512], F32)); rang = es.enter_context(ST("rang", [128, 512], F32))
        rr1 = es.enter_context(ST("rr1", [128, 512], F32)); rki = es.enter_context(ST("rki", [128, 512], I32))
        rtab = [es.enter_context(ST("rtab%d" % i, [128, 4, 512], F32)) for i in range(2)]
        rsem2 = K.dsem("ropeout")
        for c in range(4):
            K.op(DVE, lambda c=c: DVE.e.tensor_copy(out=rposf[R64s, :], in_=rposi[R64s, c, :]), R=["rposi"], W=["rposf"])
            K.op(DVE, lambda: DVE.e.tensor_scalar(out=rang[R64s, :], in0=rposf[R64s, :], scalar1=invf[R64s, 0:1], scalar2=None, op0=ALU.mult), R=["rposf", "invf"], W=["rang"])
            for kind_, shift in ((0, 0.0), (1, math.pi / 2)):
                K.op(DVE, lambda shift=shift: DVE.e.tensor_scalar(out=rr1[R64s, :], in0=rang[R64s, :], scalar1=1.0 / TWO_PI, scalar2=shift / TWO_PI, op0=ALU.mult, op1=ALU.add), R=["rang"], W=["rr1"])
                K.op(DVE, lambda: DVE.e.tensor_copy(out=rki[R64s, :], in_=rr1[R64s, :]), R=["rr1"], W=["rki"])
                K.op(DVE, lambda: DVE.e.tensor_copy(out=rr1[R64s, :], in_=rki[R64s, :]), R=["rki"], W=["rr1"])
                K.op(DVE, lambda: DVE.e.scalar_tensor_tensor(out=rr1[R64s, :], in0=rr1[R64s, :], scalar=-TWO_PI, in1=rang[R64s, :], op0=ALU.mult, op1=ALU.add), R=["rr1", "rang"], W=["rr1"])
                K.op(DVE, lambda shift=shift: DVE.e.tensor_scalar(out=rr1[R64s, :], in0=rr1[R64s, :], scalar1=shift, scalar2=PI_LO, op0=ALU.add, op1=ALU.min), R=["rr1"], W=["rr1"])
                K.op(DVE, lambda: DVE.e.tensor_scalar(out=rr1[R64s, :], in0=rr1[R64s, :], scalar1=-PI_LO, scalar2=None, op0=ALU.max), R=["rr1"], W=["rr1"])
                K.op(ACT, lambda kind_=kind_, c=c: ACT.e.activation(out=rtab[kind_][R64s, c, :], in_=rr1[R64s, :], func=AF.Sin), R=["rr1"], W=[("rtab", kind_, c)])
                K.dma(ACT, rope_d[kind_, :, c * 512:(c + 1) * 512], rtab[kind_][R64s, c, :], R=[("rtab", kind_, c)], W=["rope_d"], sem=rsem2)
        gsem = K.dsem("gd")
        for kind in range(2):
            K.dma(ACT, gd_d[kind].rearrange("h p m -> p h m"), G[:, kind, :, :], R=["G"], W=["gd"], sem=gsem)
        gsem2 = K.dsem("gd2")
        for kind in range(2):
            for h in range(8):
                off = ((kind * 8 + h) * 128) * 256 + 128
                skew = bass.AP(gd_d.tensor, off, [[255, 128], [1, 128]])
                K.dma(ACT, EB[:, kind, h, :], skew, R=["gd"], W=["EB"], sem=gsem2)
        g1_ = lambda n_: gcol[:, 0, :].unsqueeze(2).broadcast_to([128, 8, n_])
        ci = 0

        def wcast(dst, c0, n_, neg=False):
            nonlocal ci
            E_ = DVE if (ci % 2 == 0 or ci == 11) else POOL
            if neg:
                K.op(E_, lambda: E_.e.scalar_tensor_tensor(out=dst, in0=wstg[:, :, c0:c0 + n_], scalar=-1.0, in1=g1_(n_), op0=ALU.mult, op1=ALU.mult), R=["wstg", "gcol"], W=[("ring", -1)])
            else:
                K.op(E_, lambda: E_.e.tensor_tensor(out=dst, in0=wstg[:, :, c0:c0 + n_], in1=g1_(n_), op=ALU.mult), R=["wstg", "gcol"], W=[("ring", -1)])

        for j in range(12):
            rv = ring[j][:].rearrange("p (k c) -> p k c", k=8)
            if j < 4:
                wcast(rv[:, :, 0:64], j * 64, 64)
                wcast(rv[:, :, 64:128], (4 + j) * 64, 64)
            elif j < 11:
                wcast(rv, 512 + (j - 4) * 128, 128)
            else:
                wcast(rv[:, :, 0:64], 1344, 64)
                wcast(rv[:, :, 64:96], 1408, 32)
                wcast(rv[:, :, 96:112], 1424, 16, neg=True)
                wcast(rv[:, :, 112:128], 1408, 16)
            K.lw[("ring", j)] = K.lw[("ring", -1)]
            ci += 1
        snap0 = K.snapshot()
        K.barrier(snap0, keep=isring)

    def rstd_from_ss(ss_ap, rstd_ap, n, dim, keyR, keyW, tmpk):
        K.op(ACT, lambda: ACT.e.activation(out=rstd_ap, in_=ss_ap, func=AF.Ln, scale=1.0 / dim, bias=EPS), R=keyR, W=keyW)
        K.op(ACT, lambda: ACT.e.activation(out=rstd_ap, in_=rstd_ap, func=AF.Exp, scale=-0.5), R=keyW, W=keyW)

    evac_flip = {"n": 0}

    def evac(out_ap, in_ap, R, W, eng=None):
        if eng is None:
            eng = ACT if (evac_flip["n"] % 2 == 0) else DVE
            evac_flip["n"] += 1
        if eng is ACT:
            K.op(ACT, lambda: ACT.e.activation(out=out_ap, in_=in_ap, func=AF.Copy), R=R, W=W)
        else:
            K.op(DVE, lambda: DVE.e.tensor_copy(out=out_ap, in_=in_ap), R=R, W=W)

    bank = {"n": 0}

    def nbank():
        b = bank["n"] % 6
        bank["n"] += 1
        return b

    def norm_tiles(tiles, hT, hkey, junk, hb, stats=True):
        t0_, t1_ = tiles[0], tiles[-1] + 1
        if stats:
            for t in tiles:
                K.op(ACT, lambda t=t: ACT.e.activation(out=junk[:], in_=x_sb[:, t, :], func=AF.Square, accum_out=ss[:, t:t + 1]), R=[("x", t)], W=["junk", ("ss", t)])
            rstd_from_ss(ss[:, t0_:t1_], rstd[:, t0_:t1_], len(tiles), D, [("ss", t) for t in tiles], [("rstd", t) for t in tiles], None)
        for i, t in enumerate(tiles):
            hbuf = hb[i % 2]
            K.op(DVE, lambda t=t, hbuf=hbuf: DVE.e.tensor_scalar(out=hbuf[:], in0=x_sb[:, t, :], scalar1=rstd[:, t:t + 1], scalar2=None, op0=ALU.mult), R=[("x", t), ("rstd", t)], W=[("hb", i % 2)])
            pb = i % 2
            K.tr(R=[("hb", i % 2), "ident"], W=[("psT", pb)], trs=[(psT[:, pb, k * 128:(k + 1) * 128], hbuf[:, k * 128:(k + 1) * 128]) for k in range(8)], ident=ident[:])
            dst = hT(i)
            evac(dst, psT[:, pb, :].rearrange("p (k c) -> p k c", k=8), R=[("psT", pb)], W=[hkey(i)])

    def dbg_dump(name, src_ap, R):
        if dbg and name in dbg_d:
            s_ = K.dsem("dbg")
            K.dma(SP, dbg_d[name], src_ap, R=R, W=(), sem=s_)

    def dump_bf(name, ap_bf, shape, R, es_):
        if not (dbg and name in dbg_d):
            return
        tmp = es_.enter_context(ST("dbgt_" + name, shape, F32))
        K.op(DVE, lambda: DVE.e.tensor_copy(out=tmp[:], in_=ap_bf), R=R, W=["dbgt_" + name])
        s_ = K.dsem("dbg")
        K.dma(SP, dbg_d[name], tmp[:], R=["dbgt_" + name], W=(), sem=s_)

    for l in range(nlayers):
        esL = ExitStack()
        AL = lambda nm, shp, dtp: esL.enter_context(ST("%s_l%d" % (nm, l), shp, dtp))
        qT = AL("qT", [128, 4, S], BF16)
        uT = AL("uT", [128, 2, S], BF16)
        cqT = AL("cqT", [128, 2, S], BF16)
        ckvT = AL("ckvT", [128, S], BF16)
        krT = AL("krT", [128, S], BF16)
        krotT = AL("krotT", [128, S], BF16)
        es_bc = AL("es_bc", [128, 8], F32)
        lps = K.dsem("lparams")
        K.dma(SP, es_bc[:], sinks_d[l].partition_broadcast(128), R=(), W=["es_bc"], sem=lps)
        K.op(ACT, lambda: ACT.e.activation(out=es_bc[:], in_=es_bc[:], func=AF.Exp), R=["es_bc"], W=["es_bc"])
        dcol = AL("dcol", [128, 2], F32)
        wglu = AL("wglu", [128, 2, 256], BF16)
        are = AL("are", [128, 8], F32); aim = AL("aim", [128, 8], F32); ldt = AL("ldt", [128, 8], F32)
        Bre = AL("Bre", [128, 8, 16], F32); Bim = AL("Bim", [128, 8, 16], F32)
        Cre = AL("Cre", [128, 8, 16], F32); Cim = AL("Cim", [128, 8, 16], F32)
        sps = K.dsem("ssmp")
        for gl in range(2):
            pr = slice(gl * 64, (gl + 1) * 64)
            K.dma(SP, are[pr, :], bass.AP(are_d.tensor, (l * 16 + gl) * 64, [[1, 64], [128, 8]]), R=(), W=["ssmp"], sem=sps)
            K.dma(SP, aim[pr, :], bass.AP(aim_d.tensor, (l * 16 + gl) * 64, [[1, 64], [128, 8]]), R=(), W=["ssmp"], sem=sps)
            K.dma(SP, ldt[pr, :], bass.AP(ldt_d.tensor, l * 16 + gl, [[0, 64], [2, 8]]), R=(), W=["ssmp"], sem=sps)
            K.dma(SP, Bre[pr, :, :], bass.AP(bre_d.tensor, (l * 16 + gl) * 1024, [[16, 64], [2048, 8], [1, 16]]), R=(), W=["ssmp"], sem=sps)
            K.dma(SP, Bim[pr, :, :], bass.AP(bim_d.tensor, (l * 16 + gl) * 1024, [[16, 64], [2048, 8], [1, 16]]), R=(), W=["ssmp"], sem=sps)
            for tp in range(8):
                K.dma(SP, Cre[pr, tp, :], bass.AP(cre_d.tensor, (l * 16 + 2 * tp + gl) * 1024, [[1, 64], [64, 16]]), R=(), W=["ssmp"], sem=sps)
                K.dma(SP, Cim[pr, tp, :], bass.AP(cim_d.tensor, (l * 16 + 2 * tp + gl) * 1024, [[1, 64], [64, 16]]), R=(), W=["ssmp"], sem=sps)
        K.dma(SP, dcol[:], sd_d[l].rearrange("(h p) -> p h", p=128), R=(), W=["ssmp"], sem=sps)
        K.dma(POOL, wglu[:], wglu_d[l].rearrange("(k p) c -> p k c", p=128), R=(), W=["wglu"], sem=K.dsem("wglu"))


        wq = AL("wq", [128, 2, 384], BF16)
        wkv = AL("wkv", [128, 512], BF16)
        gqc = AL("gqc", [128, 2], F32); gkvc = AL("gkvc", [128, 1], F32)
        mps = K.dsem("mlap")
        K.dma(SP, gqc[:], gq_d[l].rearrange("(k p) -> p k", p=128), R=(), W=["mlap"], sem=mps)
        K.dma(SP, gkvc[:], gkv_d[l].rearrange("(k p) -> p k", p=128), R=(), W=["mlap"], sem=mps)
        mps2 = K.dsem("mlaw")
        K.dma(POOL, wq[:], wq_d[l].rearrange("(k p) c -> p k c", p=128), R=(), W=["wq"], sem=mps2)
        K.dma(POOL, wkv[:], wkv_d[l], R=(), W=["wkv"], sem=K.dsem("mlaw2"))
        K.op(POOL, lambda: POOL.e.tensor_tensor(out=wq[:], in0=wq[:], in1=gqc[:].unsqueeze(2).broadcast_to([128, 2, 384]), op=ALU.mult), R=["wq", "mlap"], W=["wq"])
        K.op(POOL, lambda: POOL.e.tensor_scalar(out=wkv[:], in0=wkv[:], scalar1=gkvc[:, 0:1], scalar2=None, op0=ALU.mult), R=["wkv", "mlap"], W=["wkv"])
        esS = ExitStack()
        kT = esS.enter_context(ST("kT_l%d" % l, [128, S], BF16))
        v_a = esS.enter_context(ST("v_a_l%d" % l, [128, NT, 128], BF16))
        with ExitStack() as es:
            hTb = [es.enter_context(ST("hTb%d" % i, [128, 8, 512], BF16)) for i in range(2)]
            junk = es.enter_context(ST("junk", [128, D], BF16))
            hb = [es.enter_context(ST("hb%d" % i, [128, D], BF16)) for i in range(2)]
            units = list(range(12))
            def stats_chunk(c):
                tl = list(range(4 * c, 4 * c + 4))
                for t in tl:
                    K.op(ACT, lambda t=t: ACT.e.activation(out=junk[:], in_=x_sb[:, t, :], func=AF.Square, accum_out=ss[:, t:t + 1]), R=[("x", t)], W=["junk", ("ss", t)])
                rstd_from_ss(ss[:, 4 * c:4 * c + 4], rstd[:, 4 * c:4 * c + 4], 4, D, [("ss", t) for t in tl], [("rstd", t) for t in tl], None)

            def norm_chunk(c):
                hT_ = hTb[c % 2]
                norm_tiles(list(range(4 * c, 4 * c + 4)), lambda i, hT_=hT_: hT_[:, :, i * 128:(i + 1) * 128], lambda i, c=c: ("hT", c % 2, i), junk, hb, stats=False)

            stats_chunk(0)
            stats_chunk(1)
            norm_chunk(0)
            for c in range(4):
                hT = hTb[c % 2]
                if c + 2 < 4:
                    stats_chunk(c + 2)
                if c + 1 < 4:
                    norm_chunk(c + 1)
                hR = [("hT", c % 2, i) for i in range(4)]
                tok = slice(c * 512, (c + 1) * 512)
                for j in range(12):
                    sl = units[j]
                    rv = ring[sl][:].rearrange("p (k c) -> p k c", k=8)
                    if j == 5:
                        b = nbank()
                        for i in range(4):
                            K.mm(R=hR + [("ring", sl)], W=[("ps", b)], mms=[(ps[:, b, i * 128:(i + 1) * 128], hT[:, k, i * 128:(i + 1) * 128], rv[:, k, :], k == 0, k == 7) for k in range(8)])
                        evac(v_a[:, 4 * c:4 * c + 4, :], ps[:, b, :].rearrange("p (a b) -> p a b", a=4), R=[("ps", b)], W=[("v_a", c)])
                        continue
                    if j == 11:
                        for (dst, c0, key) in ((krT, 0, "krT"), (krotT, 32, "krotT")):
                            b = nbank()
                            K.mm(R=hR + [("ring", sl)], W=[("ps", b)], mms=[(ps[0:96, b, :], rv[:, k, c0:c0 + 96], hT[:, k, :], k == 0, k == 7) for k in range(8)])
                            evac(dst[64:96, tok], ps[64:96, b, :], R=[("ps", b)], W=[(key, c)])
                        continue
                    b = nbank()
                    K.mm(R=hR + [("ring", sl)], W=[("ps", b)], mms=[(ps[:, b, :], rv[:, k, :], hT[:, k, :], k == 0, k == 7) for k in range(8)])
                    if j < 4:
                        dst, key = qT[:, j, tok], ("qT", j, c)
                    elif j == 4:
                        dst, key = kT[:, tok], ("kT", c)
                    elif j < 8:
                        dst, key = uT[:, j - 6, tok], ("uT", j - 6, c)
                    elif j < 10:
                        dst, key = cqT[:, j - 8, tok], ("cqT", j - 8, c)
                    else:
                        dst, key = ckvT[:, tok], ("ckvT", c)
                    evac(dst, ps[:, b, :], R=[("ps", b)], W=[key])
            K.barrier()
            if dbg and l == dbg.get("_layer", 0):
                dump_bf("qT", qT[:].rearrange("p a s -> p (a s)"), [128, 4 * S], [], es)
                dump_bf("kT", kT[:], [128, S], [], es)
                dump_bf("v_a", v_a[:].rearrange("p a s -> p (a s)"), [128, NT * 128], [], es)
                dump_bf("uT", uT[:].rearrange("p a s -> p (a s)"), [128, 2 * S], [], es)
                dump_bf("cqT", cqT[:].rearrange("p a s -> p (a s)"), [128, 2 * S], [], es)
                dump_bf("ckvT", ckvT[:], [128, S], [], es)
                dump_bf("krT", krT[64:96, :], [32, S], [], es)
                dump_bf("krotT", krotT[64:96, :], [32, S], [], es)
                K.barrier()

        wo_units = []
        for k in range(8):
            sl = k
            wo_units.append(sl)
            if k < 4:
                wload(wout_d[l, k * 64:(k + 1) * 64, :], sl, ring[sl][0:64, :])
                wload(wout_d[l, (4 + k) * 64:(5 + k) * 64, :], sl, ring[sl][64:128, :])
            else:
                wload(wout_d[l, k * 128:(k + 1) * 128, :], sl)

        with ExitStack() as es:
            T_ = lambda nm, shp, dtp=F32: es.enter_context(ST(nm, shp, dtp))
            rr = T_("rr", [128, 8]); cL = T_("cL", [128, 8]); sL = T_("sL", [128, 8]); nsL = T_("nsL", [128, 8])
            cosT = T_("cosT", [128, 8, 129]); sinT = T_("sinT", [128, 8, 129]); nsinT = T_("nsinT", [128, 8, 128])
            LBr = T_("LBr", [128, 8, 128], BF16); LBi = T_("LBi", [128, 8, 128], BF16)
            LCr = T_("LCr", [128, 8, 128], BF16); LCi = T_("LCi", [128, 8, 128], BF16)
            rec = []
            _real = (K.op, K.tr, K.mm, K.dma)
            K.op = lambda *a_, **k_: rec.append((_real[0], a_, k_))
            K.tr = lambda *a_, **k_: rec.append((_real[1], a_, k_))
            K.mm = lambda *a_, **k_: rec.append((_real[2], a_, k_))
            K.dma = lambda *a_, **k_: rec.append((_real[3], a_, k_))
            est = ExitStack()
            Tt = lambda nm, shp, dtp=F32: est.enter_context(ST(nm, shp, dtp))
            dtt = Tt("dtt", [128, 8]); ar = Tt("ar", [128, 8]); th = Tt("th", [128, 8])
            cth = Tt("cth", [128, 8]); sth = Tt("sth", [128, 8]); t8a = Tt("t8a", [128, 8]); t8b = Tt("t8b", [128, 8])
            t8i = Tt("t8i", [128, 8], I32)
            kr_ = Tt("kr_", [128, 8]); ki_ = Tt("ki_", [128, 8]); den = Tt("den", [128, 8])
            angT = Tt("angT", [128, 8, 129]); angI = Tt("angI", [128, 8, 129], I32); tmpT = Tt("tmpT", [128, 8, 129])
            Bbr = Tt("Bbr", [128, 8, 16]); Bbi = Tt("Bbi", [128, 8, 16]); tB = Tt("tB", [128, 8, 16])
            BX = Tt("BX", [128, 8, 128], BF16)
            V = DVE

            def dv(fn, R, W):
                K.op(DVE, fn, R=R, W=W)

            def sincos(ang, out_s, out_c, tmpf, tmpi, keyp):
                for (dst, shift) in ((out_s, 0.0), (out_c, math.pi / 2)):
                    dv(lambda shift=shift: V.e.tensor_scalar(out=tmpf, in0=ang, scalar1=1.0 / TWO_PI, scalar2=shift / TWO_PI, op0=ALU.mult, op1=ALU.add), [keyp + "ang"], [keyp + "tf"])
                    dv(lambda: V.e.tensor_copy(out=tmpi, in_=tmpf), [keyp + "tf"], [keyp + "ti"])
                    dv(lambda: V.e.tensor_copy(out=tmpf, in_=tmpi), [keyp + "ti"], [keyp + "tf"])
                    dv(lambda: V.e.scalar_tensor_tensor(out=tmpf, in0=tmpf, scalar=-TWO_PI, in1=ang, op0=ALU.mult, op1=ALU.add), [keyp + "tf", keyp + "ang"], [keyp + "tf"])
                    dv(lambda shift=shift: V.e.tensor_scalar(out=tmpf, in0=tmpf, scalar1=shift, scalar2=PI_LO, op0=ALU.add, op1=ALU.min), [keyp + "tf"], [keyp + "tf"])
                    dv(lambda: V.e.tensor_scalar(out=tmpf, in0=tmpf, scalar1=-PI_LO, scalar2=None, op0=ALU.max), [keyp + "tf"], [keyp + "tf"])
                    K.op(ACT, lambda dst=dst: ACT.e.activation(out=dst, in_=tmpf, func=AF.Sin), R=[keyp + "tf"], W=[keyp + "o" + str(shift)])

            P_ = ["ssmp"]
            K.op(ACT, lambda: ACT.e.activation(out=dtt[:], in_=ldt[:], func=AF.Exp), R=P_, W=["dtt"])
            dv(lambda: V.e.tensor_tensor(out=ar[:], in0=are[:], in1=dtt[:], op=ALU.mult), P_ + ["dtt"], ["ar"])
            dv(lambda: V.e.tensor_tensor(out=th[:], in0=aim[:], in1=dtt[:], op=ALU.mult), P_ + ["dtt"], ["s8ang"])
            K.op(ACT, lambda: ACT.e.activation(out=rr[:], in_=ar[:], func=AF.Exp), R=["ar"], W=["rr"])
            dv(lambda: V.e.tensor_tensor(out=angT[:], in0=th[:].unsqueeze(2).broadcast_to([128, 8, 129]), in1=iota_i[:].unsqueeze(1).broadcast_to([128, 8, 129]), op=ALU.mult), ["s8ang", "iota_i"], ["Tang"])
            sincos(angT[:], sinT[:], cosT[:], tmpT[:], angI[:], "T")
            TT = ["To0.0", "To" + str(math.pi / 2)]
            S8 = ["sth", "cth"]
            CL = ["sL", "cL"]
            dv(lambda: V.e.tensor_copy(out=sth[:], in_=sinT[:, :, 1]), TT, ["sth"])
            dv(lambda: V.e.tensor_copy(out=cth[:], in_=cosT[:, :, 1]), TT, ["cth"])
            dv(lambda: V.e.tensor_copy(out=sL[:], in_=sinT[:, :, 128]), TT, ["sL"])
            dv(lambda: V.e.tensor_copy(out=cL[:], in_=cosT[:, :, 128]), TT, ["cL"])
            dv(lambda: V.e.tensor_scalar(out=nsL[:], in0=sL[:], scalar1=-1.0, scalar2=None, op0=ALU.mult), ["sL"], ["nsL"])
            dv(lambda: V.e.tensor_scalar(out=nsinT[:], in0=sinT[:, :, 0:128], scalar1=-1.0, scalar2=None, op0=ALU.mult), TT, ["nsinT"])
            dv(lambda: V.e.tensor_tensor(out=t8a[:], in0=rr[:], in1=cth[:], op=ALU.mult), ["rr"] + S8, ["s8tf"])
            dv(lambda: V.e.tensor_scalar(out=t8a[:], in0=t8a[:], scalar1=-1.0, scalar2=None, op0=ALU.add), ["s8tf"], ["s8tf"])
            dv(lambda: V.e.tensor_tensor(out=t8b[:], in0=rr[:], in1=sth[:], op=ALU.mult), ["rr"] + S8, ["t8b"])
            dv(lambda: V.e.tensor_tensor(out=den[:], in0=are[:], in1=are[:], op=ALU.mult), P_, ["den"])
            dv(lambda: V.e.tensor_tensor(out=kr_[:], in0=aim[:], in1=aim[:], op=ALU.mult), P_, ["kr_"])
            dv(lambda: V.e.tensor_tensor(out=den[:], in0=den[:], in1=kr_[:], op=ALU.add), ["den", "kr_"], ["den"])
            dv(lambda: V.e.reciprocal(out=den[:], in_=den[:]), ["den"], ["den"])
            dv(lambda: V.e.tensor_tensor(out=kr_[:], in0=t8a[:], in1=are[:], op=ALU.mult), ["s8tf", "kr_"] + P_, ["kr_"])
            dv(lambda: V.e.tensor_tensor(out=ki_[:], in0=t8b[:], in1=aim[:], op=ALU.mult), ["t8b"] + P_, ["ki_"])
            dv(lambda: V.e.tensor_tensor(out=kr_[:], in0=kr_[:], in1=ki_[:], op=ALU.add), ["kr_", "ki_"], ["kr_"])
            dv(lambda: V.e.tensor_tensor(out=kr_[:], in0=kr_[:], in1=den[:], op=ALU.mult), ["kr_", "den"], ["kr_"])
            dv(lambda: V.e.tensor_tensor(out=ki_[:], in0=t8b[:], in1=are[:], op=ALU.mult), ["t8b", "ki_", "kr_"] + P_, ["ki_"])
            dv(lambda: V.e.tensor_tensor(out=t8b[:], in0=t8a[:], in1=aim[:], op=ALU.mult), ["s8tf", "ki_"] + P_, ["t8b"])
            dv(lambda: V.e.tensor_tensor(out=ki_[:], in0=ki_[:], in1=t8b[:], op=ALU.subtract), ["ki_", "t8b"], ["ki_"])
            dv(lambda: V.e.tensor_tensor(out=ki_[:], in0=ki_[:], in1=den[:], op=ALU.mult), ["ki_", "den"], ["ki_"])
            krb = kr_[:].unsqueeze(2).broadcast_to([128, 8, 16])
            kib = ki_[:].unsqueeze(2).broadcast_to([128, 8, 16])
            dv(lambda: V.e.tensor_tensor(out=Bbr[:], in0=Bre[:], in1=krb, op=ALU.mult), P_ + ["kr_"], ["Bbr"])
            dv(lambda: V.e.tensor_tensor(out=tB[:], in0=Bim[:], in1=kib, op=ALU.mult), P_ + ["ki_"], ["tB"])
            dv(lambda: V.e.tensor_tensor(out=Bbr[:], in0=Bbr[:], in1=tB[:], op=ALU.subtract), ["Bbr", "tB"], ["Bbr"])
            dv(lambda: V.e.tensor_tensor(out=Bbi[:], in0=Bim[:], in1=krb, op=ALU.mult), P_ + ["kr_"], ["Bbi"])
            dv(lambda: V.e.tensor_tensor(out=tB[:], in0=Bre[:], in1=kib, op=ALU.mult), P_ + ["ki_", "Bbr"], ["tB"])
            dv(lambda: V.e.tensor_tensor(out=Bbi[:], in0=Bbi[:], in1=tB[:], op=ALU.add), ["Bbi", "tB"], ["Bbi"])
            rec_split = len(rec)
            for tname, tt in (("LBr", LBr), ("LBi", LBi), ("LCr", LCr), ("LCi", LCi)):
                K.op(POOL, lambda tt=tt: POOL.e.memset(tt[:].rearrange("p a b -> p (a b)"), 0.0), W=[tname])
            for tp in range(8):
                for gl in range(2):
                    pr = slice(gl * 64, (gl + 1) * 64)
                    c0 = 32 * (tp % 4) + 16 * gl
                    K.op(POOL, lambda tp=tp, pr=pr, c0=c0: POOL.e.tensor_copy(out=LCr[pr, tp, c0:c0 + 16], in_=Cre[pr, tp, :]), R=P_ + ["LCr"], W=["LCr"])
                    K.op(POOL, lambda tp=tp, pr=pr, c0=c0: POOL.e.tensor_scalar(out=LCi[pr, tp, c0:c0 + 16], in0=Cim[pr, tp, :], scalar1=-1.0, scalar2=None, op0=ALU.mult), R=P_ + ["LCi"], W=["LCi"])
            for (src, dstL, nm) in ((Bbr, LBr, "LBr"), (Bbi, LBi, "LBi")):
                K.op(POOL, lambda: POOL.e.memset(BX[:].rearrange("p a b -> p (a b)"), 0.0), R=["BX"], W=["BX"])
                for tp in range(8):
                    for gl in range(2):
                        pr = slice(gl * 64, (gl + 1) * 64)
                        c0 = 32 * (tp % 4) + 16 * gl
                        K.op(POOL, lambda tp=tp, pr=pr, c0=c0, src=src: POOL.e.tensor_copy(out=BX[pr, tp, c0:c0 + 16], in_=src[pr, tp, :]), R=["Bbr", "Bbi", "BX"], W=["BX"])
                K.tr(R=["BX", "ident"], W=[("ps", 6)], trs=[(psT[:, 0, tp * 128:(tp + 1) * 128], BX[:, tp, :]) for tp in range(8)], ident=ident[:])
                evac(dstL[:], psT[:, 0, :].rearrange("p (a b) -> p a b", a=8), R=[("ps", 6)], W=[nm])
            K.op, K.tr, K.mm, K.dma = _real

            first_tr = next(i_ for i_, e_ in enumerate(rec) if i_ >= rec_split and e_[0] is _real[1])
            tail_len = len(rec) - first_tr

            def replay(n_, all_=False):
                for _ in range(n_):
                    if rec and (all_ or len(rec) > tail_len):
                        f_, a_, k_ = rec.pop(0)
                        f_(*a_, **k_)

            with ExitStack() as es_swa:
                e_sb = [es_swa.enter_context(ST("e_sb%d" % i, [128, 512], F32)) for i in range(2)]
                p_sb = [es_swa.enter_context(ST("p_sb%d" % i, [128, 512], BF16)) for i in range(4)]
                tmps = [es_swa.enter_context(ST("tmps%d" % i, [128, 512], F32)) for i in range(2)]
                items = [(n, g) for n in range(NT) for g in range(2)]
                sb_state = {"s": 0, "e": 0, "p": 0}
                qk_out = {}

                def swa_qk(idx):
                    n, g = items[idx]
                    rows = slice(g * 64, (g + 1) * 64)
                    kts = [n - 1, n] if n > 0 else [n]
                    qkeys = [("qT", j, n // 4) for j in range(4)]
                    res = []
                    for kt in kts:
                        b = sb_state["s"] % 4
                        sb_state["s"] += 1
                        K.mm(R=qkeys + [("kT", kt // 4)], W=[("ps", b)],
                             mms=[(ps[:, b, :].rearrange("p (a b) -> p a b", a=4), kT[rows, kt * 128:(kt + 1) * 128], qT[rows, :, n * 128:(n + 1) * 128], True, True)])
                        res.append((kt, b))
                    qk_out[idx] = res

                pl_out = {}

                def swa_em(idx):
                    n, g = items[idx]
                    pl = []
                    for (kt, b) in qk_out.pop(idx):
                        kind = 0 if kt == n - 1 else 1
                        ei_ = sb_state["e"] % 2
                        sb_state["e"] += 1
                        pi_ = sb_state["p"] % 4
                        sb_state["p"] += 1
                        eb = e_sb[ei_]
                        K.op(ACT, lambda eb=eb, b=b: ACT.e.activation(out=eb[:], in_=ps[:, b, :], func=AF.Exp, scale=0.125), R=[("ps", b)], W=[("e_sb", ei_)])
                        pb_ = p_sb[pi_]
                        K.op(DVE, lambda eb=eb, pb_=pb_, kind=kind: DVE.e.tensor_tensor(out=pb_[:].rearrange("p (a b) -> p a b", a=4), in0=eb[:].rearrange("p (a b) -> p a b", a=4), in1=EB[:, kind, 4 * g:4 * g + 4, :], op=ALU.mult),
                             R=[("e_sb", ei_), "EB"], W=[("p_sb", pi_)])
                        pl.append((pb_, ("p_sb", pi_), kt))
                    pl_out[idx] = pl

                def swa_fin(idx):
                    n, g = items[idx]
                    rows = slice(g * 64, (g + 1) * 64)
                    pl = pl_out.pop(idx)
                    bn = 4 + 2 * (idx % 2)
                    bs = bn + 1
                    K.mm(R=[k_ for (_, k_, _) in pl] + [("v_a", kt // 4) for (_, _, kt) in pl], W=[("ps", bn)],
                         mms=[(ps[:, bn, :], v_a[:, kt, :], pb_[:], i == 0, i == len(pl) - 1) for i, (pb_, _, kt) in enumerate(pl)])
                    K.mm(R=[k_ for (_, k_, _) in pl] + ["ones"], W=[("ps", bs)],
                         mms=[(ps[:, bs, :], ones[:], pb_[:], i == 0, i == len(pl) - 1) for i, (pb_, _, kt) in enumerate(pl)])
                    tm = tmps[g]
                    for c_ in range(4):
                        K.op(ACT, lambda tm=tm, bs=bs, rows=rows, g=g, c_=c_: ACT.e.activation(out=tm[rows, c_ * 128:(c_ + 1) * 128], in_=ps[rows, bs, c_ * 128:(c_ + 1) * 128], func=AF.Ln, bias=es_bc[rows, 4 * g + c_:4 * g + c_ + 1]),
                             R=[("ps", bs), "es_bc"], W=[("tmps", g)])
                    K.op(ACT, lambda tm=tm, rows=rows: ACT.e.activation(out=tm[rows, :], in_=tm[rows, :], func=AF.Exp, scale=-1.0), R=[("tmps", g)], W=[("tmps", g)])
                    K.op(DVE, lambda tm=tm, bn=bn, rows=rows, n=n: DVE.e.tensor_tensor(out=qT[rows, :, n * 128:(n + 1) * 128], in0=ps[rows, bn, :].rearrange("p (a b) -> p a b", a=4), in1=tm[rows, :].rearrange("p (a b) -> p a b", a=4), op=ALU.mult),
                         R=[("ps", bn), ("tmps", g)], W=[("oa", n, g)])

                NIT = len(items)
                swa_qk(0)
                swa_qk(1)
                swa_em(0)
                for idx in range(NIT):
                    if idx + 2 < NIT:
                        swa_qk(idx + 2)
                    if idx + 1 < NIT:
                        swa_em(idx + 1)
                    swa_fin(idx)
                    replay(5)
                K.barrier()
                if dbg and l == dbg.get("_layer", 0):
                    dump_bf("oaT", qT[:].rearrange("p a s -> p (a s)"), [128, 4 * S], [], es_swa)
                    K.barrier()

            replay(len(rec), all_=True)
            K.barrier()
            est.close()

            xm_re = [T_("xm_re%d" % i, [128, 4, 128]) for i in range(2)]; xm_im = [T_("xm_im%d" % i, [128, 4, 128]) for i in range(2)]
            _xt1 = T_("xt1_0", [128, 4, 128]); _xt2 = T_("xt2_0", [128, 4, 128])
            xt1 = [_xt1, _xt1]; xt2 = [_xt2, _xt2]
            q_re = [T_("q_re%d" % i, [128, 4, 128]) for i in range(2)]
            q_im = [T_("q_im%d" % i, [128, 4, 128]) for i in range(2)]
            zt1 = [T_("zt1_%d" % i, [128, 4, 128], BF16) for i in range(2)]; zt2 = [T_("zt2_%d" % i, [128, 4, 128], BF16) for i in range(2)]
            zt3 = [T_("zt3_%d" % i, [128, 4, 128], BF16) for i in range(2)]; zt4 = [T_("zt4_%d" % i, [128, 4, 128], BF16) for i in range(2)]
            ini_re = T_("ini_re", [128, 8]); ini_im = T_("ini_im", [128, 8]); it1 = T_("it1", [128, 4]); it2 = T_("it2", [128, 4])
            yc = T_("yc", [128, 2, 128]); ygc = T_("ygc", [128, 2, 128]); ygb = T_("ygb", [128, 2, 128], BF16); sgc = T_("sgc", [128, 2, 128])
            PO = POOL
            iters = [(n, hf) for n in range(NT) for hf in range(2)]

            def st0(k):
                n, hf = iters[k]
                tok = slice(n * 128, (n + 1) * 128)
                tps = list(range(4 * hf, 4 * hf + 4))
                t4 = slice(4 * hf, 4 * hf + 4)
                bre = 2 * hf
                bim = 2 * hf + 1
                K.mm(R=[("uT", hf, n // 4), "LBr"], W=[("ps", bre)],
                     mms=[(ps[:, bre, i * 128:(i + 1) * 128], LBr[:, tp, :], uT[:, hf, tok], True, True) for i, tp in enumerate(tps)])
                K.mm(R=[("uT", hf, n // 4), "LBi"], W=[("ps", bim)],
                     mms=[(ps[:, bim, i * 128:(i + 1) * 128], LBi[:, tp, :], uT[:, hf, tok], True, True) for i, tp in enumerate(tps)])

            def st0b(k):
                n, hf = iters[k]
                t4 = slice(4 * hf, 4 * hf + 4)
                bre = 2 * hf
                bim = 2 * hf + 1
                xre_v = ps[:, bre, :].rearrange("p (a b) -> p a b", a=4)
                xim_v = ps[:, bim, :].rearrange("p (a b) -> p a b", a=4)
                dv(lambda: V.e.tensor_tensor(out=xm_re[hf][:], in0=xre_v, in1=cosT[:, t4, 0:128], op=ALU.mult), [("ps", bre)] + TT, [("xm_re", hf)])
                dv(lambda: V.e.tensor_tensor(out=xt1[hf][:], in0=xim_v, in1=sinT[:, t4, 0:128], op=ALU.mult), [("ps", bim)] + TT, ["xt1"])
                dv(lambda: V.e.tensor_tensor(out=xm_im[hf][:], in0=xim_v, in1=cosT[:, t4, 0:128], op=ALU.mult), [("ps", bim)] + TT, [("xm_im", hf)])
                dv(lambda: V.e.tensor_tensor(out=xt2[hf][:], in0=xre_v, in1=nsinT[:, t4, :], op=ALU.mult), [("ps", bre), "nsinT"], ["xt2"])
                K.op(PO, lambda: PO.e.tensor_tensor(out=xm_re[hf][:], in0=xm_re[hf][:], in1=xt1[hf][:], op=ALU.add), R=[("xm_re", hf), "xt1"], W=[("xm_re", hf)])
                K.op(PO, lambda: PO.e.tensor_tensor(out=xm_im[hf][:], in0=xm_im[hf][:], in1=xt2[hf][:], op=ALU.add), R=[("xm_im", hf), "xt2"], W=[("xm_im", hf)])

            def st1(k):
                n, hf = iters[k]
                tps = list(range(4 * hf, 4 * hf + 4))
                t4 = slice(4 * hf, 4 * hf + 4)
                qr = q_re[hf]; qi = q_im[hf]
                for i, tp in enumerate(tps):
                    init_r = 0.0 if n == 0 else ini_re[:, tp:tp + 1]
                    init_i = 0.0 if n == 0 else ini_im[:, tp:tp + 1]
                    dv(lambda i=i, tp=tp, init_r=init_r: V.e.tensor_tensor_scan(out=qr[:, i, :], data0=rr[:, tp:tp + 1].broadcast_to([128, 128]), data1=xm_re[hf][:, i, :], initial=init_r, op0=ALU.mult, op1=ALU.add),
                       [("xm_re", hf), "rr", ("ini_re", hf)], [("q_re", hf)])
                    dv(lambda i=i, tp=tp, init_i=init_i: V.e.tensor_tensor_scan(out=qi[:, i, :], data0=rr[:, tp:tp + 1].broadcast_to([128, 128]), data1=xm_im[hf][:, i, :], initial=init_i, op0=ALU.mult, op1=ALU.add),
                       [("xm_im", hf), "rr", ("ini_im", hf)], [("q_im", hf)])
                QR = [("q_re", hf)]
                QI = [("q_im", hf)]
                if n < NT - 1:
                    dv(lambda: V.e.tensor_tensor(out=it1[:], in0=qr[:, :, 127], in1=cL[:, t4], op=ALU.mult), QR + CL, ["it1"])
                    dv(lambda: V.e.tensor_tensor(out=it2[:], in0=qi[:, :, 127], in1=nsL[:, t4], op=ALU.mult), QI + ["nsL"], ["it2"])
                    dv(lambda: V.e.tensor_tensor(out=ini_re[:, t4], in0=it1[:], in1=it2[:], op=ALU.add), ["it1", "it2"], [("ini_re", hf)])
                    dv(lambda: V.e.tensor_tensor(out=it1[:], in0=qr[:, :, 127], in1=sL[:, t4], op=ALU.mult), QR + CL + ["it1"], ["it1"])
                    dv(lambda: V.e.tensor_tensor(out=it2[:], in0=qi[:, :, 127], in1=cL[:, t4], op=ALU.mult), QI + CL + ["it2"], ["it2"])
                    dv(lambda: V.e.tensor_tensor(out=ini_im[:, t4], in0=it1[:], in1=it2[:], op=ALU.add), ["it1", "it2"], [("ini_im", hf)])
                z1, z2, z3, z4 = zt1[hf], zt2[hf], zt3[hf], zt4[hf]
                dv(lambda: V.e.tensor_tensor(out=z1[:], in0=qr[:], in1=cosT[:, t4, 0:128], op=ALU.mult), QR + TT, [("zt1", hf)])
                dv(lambda: V.e.tensor_tensor(out=z2[:], in0=qi[:], in1=nsinT[:, t4, :], op=ALU.mult), QI + ["nsinT"], [("zt2", hf)])
                dv(lambda: V.e.tensor_tensor(out=z3[:], in0=qr[:], in1=sinT[:, t4, 0:128], op=ALU.mult), QR + TT, [("zt3", hf)])
                dv(lambda: V.e.tensor_tensor(out=z4[:], in0=qi[:], in1=cosT[:, t4, 0:128], op=ALU.mult), QI + TT, [("zt4", hf)])
                zl = [(LCr, z1), (LCr, z2), (LCi, z3), (LCi, z4)]
                K.mm(R=[("zt1", hf), ("zt2", hf), ("zt3", hf), ("zt4", hf), "LCr", "LCi"], W=[("ps4", hf)],
                     mms=[(ps[:, 4, hf * 128:(hf + 1) * 128], LC_[:, tp, :], z_[:, i, :], (i == 0 and ri == 0), (i == 3 and ri == 3))
                          for i, tp in enumerate(tps) for ri, (LC_, z_) in enumerate(zl)])

            def st2(k):
                n, hf = iters[k]
                tok = slice(n * 128, (n + 1) * 128)
                dv(lambda: V.e.scalar_tensor_tensor(out=yc[:, hf, :], in0=uT[:, hf, tok], scalar=dcol[:, hf:hf + 1], in1=ps[:, 4, hf * 128:(hf + 1) * 128], op0=ALU.mult, op1=ALU.add),
                   [("ps4", hf), ("uT", hf, n // 4), "ssmp"], [("yc", hf)])
                if hf == 1:
                    if dbg and l == dbg.get("_layer", 0) and "yT" in dbg_d:
                        s_ = K.dsem("dbg")
                        K.dma(SP, dbg_d["yT"][:, :, tok], yc[:], R=[("yc", 0), ("yc", 1)], W=(), sem=s_)
                    K.op(ACT, lambda: ACT.e.activation(out=ygc[:], in_=yc[:], func=AF.Gelu_apprx_tanh), R=[("yc", 0), ("yc", 1)], W=["ygc"])
                    K.op(ACT, lambda: ACT.e.activation(out=ygb[:], in_=ygc[:], func=AF.Copy), R=["ygc"], W=["ygb"])
                    for h2 in range(2):
                        K.mm(R=["ygb", "wglu"], W=[("ps5", h2)], mms=[(ps[:, 5, h2 * 128:(h2 + 1) * 128], wglu[:, kk, h2 * 128:(h2 + 1) * 128], ygb[:, kk, :], kk == 0, kk == 1) for kk in range(2)])
                    K.op(ACT, lambda: ACT.e.activation(out=sgc[:], in_=ps[:, 5, 0:256].rearrange("p (a b) -> p a b", a=2), func=AF.Sigmoid), R=[("ps5", 0), ("ps5", 1)], W=["sgc"])

            def st3(n):
                tok = slice(n * 128, (n + 1) * 128)
                dv(lambda: V.e.tensor_tensor(out=uT[:, :, tok], in0=sgc[:], in1=ygc[:], op=ALU.mult), ["sgc", "ygc"], [("ob", n)])

            NI = len(iters)
            st0(0)
            st0b(0)
            st0(1)
            for k in range(NI):
                if k + 2 < NI:
                    st0(k + 2)
                if k + 1 < NI:
                    st0b(k + 1)
                st1(k)
                if k >= 1:
                    st2(k - 1)
                if k >= 2 and iters[k - 2][1] == 1:
                    st3(iters[k - 2][0])
            st2(NI - 1)
            st3(iters[NI - 1][0])
            K.barrier()
            if dbg and l == dbg.get("_layer", 0):
                dump_bf("obT", uT[:].rearrange("p a s -> p (a s)"), [128, 2 * S], [], es)
                K.barrier()

        esS.close()
        esL2 = ExitStack()
        ocT = esL2.enter_context(ST("ocT_l%d" % l, [128, 2, S], BF16))
        with ExitStack() as es:
            T_ = lambda nm, shp, dtp=F32: es.enter_context(ST(nm, shp, dtp))
            wqr = T_("wqr", [128, 2, 384], BF16)
            K.op(POOL, lambda: POOL.e.tensor_copy(out=wqr[:], in_=wq[:]), R=["wq"], W=["wqr"])
            wq4 = wq[:].rearrange("p k (h c) -> p k h c", h=4)
            wqr4 = wqr[:].rearrange("p k (h c) -> p k h c", h=4)
            for k in range(2):
                K.op(POOL, lambda k=k: POOL.e.tensor_scalar(out=wqr4[:, k, :, 64:80], in0=wq4[:, k, :, 80:96], scalar1=-1.0, scalar2=None, op0=ALU.mult), R=["wq", "wqr"], W=["wqr"])
                K.op(POOL, lambda k=k: POOL.e.tensor_copy(out=wqr4[:, k, :, 80:96], in_=wq4[:, k, :, 64:80]), R=["wq", "wqr"], W=["wqr"])
            wkv4 = wkv[:].rearrange("p (h c) -> p h c", h=4)
            R64 = slice(64, 96)
            scl = 96.0 ** -0.5
            esp = ExitStack()
            espp = ExitStack()
            Tp = lambda nm, shp, dtp=F32: esp.enter_context(ST(nm, shp, dtp))
            QT = Tp("QT", [128, 2, S], BF16)
            KT = Tp("KT", [128, 2, S], BF16)
            vm = Tp("vm", [128, NT, 128], BF16)
            eT = [Tp("eT%d" % i, [128, 512], BF16) for i in range(3)]
            rsum = [Tp("rsum%d" % i, [128, 512]) for i in range(2)]
            Tq = lambda nm, shp, dtp=F32: espp.enter_context(ST(nm, shp, dtp))
            cosRb = [Tq("cosR%d" % i, [128, 512]) for i in range(2)]; sinRb = [Tq("sinR%d" % i, [128, 512]) for i in range(2)]
            cosq = Tq("cosq", [128, 512]); sinq = Tq("sinq", [128, 512])
            sq = Tq("sq", [128, 2, 512], BF16); sqk = Tq("sqk", [128, 512], BF16)
            rq = Tq("rq", [128, 512]); rk = Tq("rk", [128, 512]); rkt = Tq("rkt", [128, 4])
            r1 = Tq("r1", [128, 512]); r2 = Tq("r2", [128, 512]); k1 = r1; k2 = r2
            kpe = Tq("kpe", [128, 512], BF16)
            for pair in range(2):
                if True:
                    if True:
                        pos_sems = [K.dsem("ropeld0"), K.dsem("ropeld1")]
                        for c in range(4):
                            tok = slice(c * 512, (c + 1) * 512)
                            cosR = cosRb[c % 2]; sinR = sinRb[c % 2]
                            ckey = ("cosR", c % 2); skey = ("sinR", c % 2)
                            K.dma(SP, sinR[R64, :], rope_d[0, :, tok], R=(), W=[skey], sem=pos_sems[c % 2])
                            K.dma(SP, cosR[R64, :], rope_d[1, :, tok], R=(), W=[ckey], sem=pos_sems[c % 2])
                            K.op(ACT, lambda tok=tok: ACT.e.activation(out=sq[:], in_=cqT[:, :, tok], func=AF.Square), R=[("cqT", 0, c), ("cqT", 1, c)], W=["sq"])
                            K.op(ACT, lambda tok=tok: ACT.e.activation(out=sqk[:], in_=ckvT[:, tok], func=AF.Square), R=[("ckvT", c)], W=["sqk"])
                            b = nbank()
                            K.mm(R=["sq", "ones"], W=[("ps", b)], mms=[(ps[:, b, :], ones[:], sq[:, k, :], k == 0, k == 1) for k in range(2)])
                            K.op(ACT, lambda b=b: ACT.e.activation(out=rq[:], in_=ps[:, b, :], func=AF.Ln, scale=1.0 / 256, bias=EPS), R=[("ps", b)], W=["rq"])
                            K.op(ACT, lambda: ACT.e.activation(out=rq[:], in_=rq[:], func=AF.Exp, scale=-0.5), R=["rq"], W=["rq"])
                            b = nbank()
                            K.mm(R=["sqk", "ones"], W=[("ps", b)], mms=[(ps[:, b, :], ones[:], sqk[:], True, True)])
                            K.op(ACT, lambda b=b: ACT.e.activation(out=rk[:], in_=ps[:, b, :], func=AF.Ln, scale=1.0 / 128, bias=EPS), R=[("ps", b)], W=["rk"])
                            K.op(ACT, lambda: ACT.e.activation(out=rk[:], in_=rk[:], func=AF.Exp, scale=-0.5), R=["rk"], W=["rk"])
                            b = nbank()
                            K.mm(R=["sqk", "ones"], W=[("ps", b)], mms=[(ps[:, b, i:i + 1], sqk[:, i * 128:(i + 1) * 128], ones[:, 0:1], True, True) for i in range(4)])
                            K.op(ACT, lambda b=b: ACT.e.activation(out=rkt[:], in_=ps[:, b, 0:4], func=AF.Ln, scale=1.0 / 128, bias=EPS), R=[("ps", b)], W=["rkt"])
                            K.op(ACT, lambda: ACT.e.activation(out=rkt[:], in_=rkt[:], func=AF.Exp, scale=-0.5), R=["rkt"], W=["rkt"])
                            K.op(POOL, lambda tok=tok, cosR=cosR: POOL.e.tensor_tensor(out=k2[R64, :], in0=krT[R64, tok], in1=cosR[R64, :], op=ALU.mult), R=[("krT", c), ckey], W=["r2"])
                            K.op(POOL, lambda tok=tok, sinR=sinR: POOL.e.tensor_tensor(out=k1[R64, :], in0=krotT[R64, tok], in1=sinR[R64, :], op=ALU.mult), R=[("krotT", c), skey], W=["r1"])
                            K.op(POOL, lambda: POOL.e.tensor_tensor(out=kpe[R64, :], in0=k2[R64, :], in1=k1[R64, :], op=ALU.add), R=["r1", "r2"], W=["kpe"])
                            K.op(DVE, lambda cosR=cosR: DVE.e.tensor_tensor(out=cosq[R64, :], in0=cosR[R64, :], in1=rq[R64, :], op=ALU.mult), R=[ckey, "rq"], W=["cosq"])
                            K.op(DVE, lambda sinR=sinR: DVE.e.tensor_tensor(out=sinq[R64, :], in0=sinR[R64, :], in1=rq[R64, :], op=ALU.mult), R=[skey, "rq"], W=["sinq"])
                            for hl in range(2):
                                h = 2 * pair + hl
                                ba = nbank()
                                K.mm(R=["wq", ("cqT", 0, c), ("cqT", 1, c)], W=[("ps", ba)], mms=[(ps[0:96, ba, :], wq[:, k, h * 96:(h + 1) * 96], cqT[:, k, tok], k == 0, k == 1) for k in range(2)])
                                bb = nbank()
                                K.mm(R=["wqr", ("cqT", 0, c), ("cqT", 1, c)], W=[("ps", bb)], mms=[(ps[0:96, bb, :], wqr[:, k, h * 96:(h + 1) * 96], cqT[:, k, tok], k == 0, k == 1) for k in range(2)])
                                K.op(DVE, lambda hl=hl, ba=ba, tok=tok: DVE.e.tensor_tensor(out=QT[0:64, hl, tok], in0=ps[0:64, ba, :], in1=rq[0:64, :], op=ALU.mult), R=[("ps", ba), "rq"], W=[("QT", hl, c)])
                                K.op(DVE, lambda ba=ba: DVE.e.tensor_tensor(out=r1[R64, :], in0=ps[R64, ba, :], in1=cosq[R64, :], op=ALU.mult), R=[("ps", ba), "cosq"], W=["r1"])
                                K.op(DVE, lambda bb=bb: DVE.e.tensor_tensor(out=r2[R64, :], in0=ps[R64, bb, :], in1=sinq[R64, :], op=ALU.mult), R=[("ps", bb), "sinq"], W=["r2"])
                                K.op(DVE, lambda hl=hl, tok=tok: DVE.e.tensor_tensor(out=QT[R64, hl, tok], in0=r1[R64, :], in1=r2[R64, :], op=ALU.add), R=["r1", "r2"], W=[("QTp", hl, c)])
                                bk_ = nbank()
                                K.mm(R=["wkv", ("ckvT", c)], W=[("ps", bk_)], mms=[(ps[0:64, bk_, :], wkv[:, h * 128:h * 128 + 64], ckvT[:, tok], True, True)])
                                K.op(DVE, lambda hl=hl, bk_=bk_, tok=tok: DVE.e.tensor_tensor(out=KT[0:64, hl, tok], in0=ps[0:64, bk_, :], in1=rk[0:64, :], op=ALU.mult), R=[("ps", bk_), "rk"], W=[("KT", hl, c)])
                                K.op(ACT, lambda hl=hl, tok=tok: ACT.e.activation(out=KT[R64, hl, tok], in_=kpe[R64, :], func=AF.Copy), R=["kpe"], W=[("KTp", hl, c)])
                            bv = nbank()
                            K.mm(R=["wkv", ("ckvT", c)], W=[("ps", bv)],
                                 mms=[(ps[:, bv, i * 128:(i + 1) * 128].rearrange("p (h d) -> p h d", h=2), ckvT[:, c * 512 + i * 128:c * 512 + (i + 1) * 128], wkv4[:, 2 * pair:2 * pair + 2, 64:128], True, True) for i in range(4)])
                            K.op(DVE, lambda bv=bv, c=c: DVE.e.tensor_tensor(out=vm[:, 4 * c:4 * c + 4, :], in0=ps[:, bv, :].rearrange("p (a b) -> p a b", a=4), in1=rkt[:, 0:4].unsqueeze(2).broadcast_to([128, 4, 128]), op=ALU.mult), R=[("ps", bv), "rkt"], W=[("vm", 4 * c + i) for i in range(4)])
                        if dbg and l == dbg.get("_layer", 0):
                            K.barrier()
                            dump_bf("QT%d" % pair, QT[0:96, :, :].rearrange("p a s -> p (a s)"), [96, 2 * S], [], espp)
                            dump_bf("KT%d" % pair, KT[0:96, :, :].rearrange("p a s -> p (a s)"), [96, 2 * S], [], espp)
                            dump_bf("vm%d" % pair, vm[:].rearrange("p a s -> p (a s)"), [128, NT * 128], [], espp)
                            K.barrier()
                    with ExitStack() as esa:
                        aitems = [(hl, Qc, j) for hl in range(2) for Qc in range(4) for j in range(4 * Qc + 4)]
                        ast = {"s": 0, "e": 0, "f": 0}
                        aqk = {}

                        def geom(Qc, j):
                            qb0 = max(j, 4 * Qc)
                            c0 = (qb0 - 4 * Qc) * 128
                            return qb0, c0, 512 - c0

                        def mla_qk(idx):
                            hl, Qc, j = aitems[idx]
                            qb0, c0, ncol = geom(Qc, j)
                            qtok = slice(qb0 * 128, (4 * Qc + 4) * 128)
                            b = ast["s"] % 2
                            ast["s"] += 1
                            K.mm(R=[("QT", hl, Qc), ("QTp", hl, Qc), ("KT", hl, j // 4), ("KTp", hl, j // 4)], W=[("ps", b)],
                                 mms=[(ps[:, b, 0:ncol], KT[0:96, hl, j * 128:(j + 1) * 128], QT[0:96, hl, qtok], True, True)])
                            aqk[idx] = b

                        def mla_rest(idx):
                            hl, Qc, j = aitems[idx]
                            qb0, c0, ncol = geom(Qc, j)
                            nj = 4 * Qc + 4
                            grp = hl * 4 + Qc
                            bn = 2 + 2 * (grp % 2)
                            bs = bn + 1
                            hr = slice(hl * 64, hl * 64 + 64)
                            b = aqk.pop(idx)
                            ei_ = ast["e"] % 3
                            ast["e"] += 1
                            et = eT[ei_]
                            K.op(ACT, lambda et=et, b=b, ncol=ncol: ACT.e.activation(out=et[:, 0:ncol], in_=ps[:, b, 0:ncol], func=AF.Exp, scale=scl), R=[("ps", b)], W=[("eT", ei_)])
                            if j >= 4 * Qc:
                                K.op(POOL, lambda et=et: POOL.e.affine_select(out=et[:, 0:128], in_=et[:, 0:128], pattern=[[1, 128]], compare_op=ALU.is_ge, fill=0.0, base=0, channel_multiplier=-1), R=[("eT", ei_)], W=[("eT", ei_)])
                            K.mm(R=[("eT", ei_), ("vm", j)], W=[("ps", bn)], mms=[(ps[:, bn, c0:512], vm[:, j, :], et[:, 0:ncol], j == 0, j == nj - 1)])
                            K.mm(R=[("eT", ei_), "ones"], W=[("ps", bs)], mms=[(ps[:, bs, c0:512], ones[:], et[:, 0:ncol], j == 0, j == nj - 1)])
                            if j == nj - 1:
                                fi_ = ast["f"] % 2
                                ast["f"] += 1
                                rs = rsum[fi_]
                                K.op(DVE, lambda rs=rs, bs=bs, hr=hr: DVE.e.reciprocal(out=rs[hr, :], in_=ps[hr, bs, :]), R=[("ps", bs)], W=[("rsum", fi_)])
                                K.op(DVE, lambda rs=rs, bn=bn, hr=hr, Qc=Qc: DVE.e.tensor_tensor(out=ocT[hr, pair, Qc * 512:(Qc + 1) * 512], in0=ps[hr, bn, :], in1=rs[hr, :], op=ALU.mult), R=[("ps", bn), ("rsum", fi_)], W=[("ocT", pair, hl, Qc)])

                        mla_qk(0)
                        for idx in range(len(aitems)):
                            if idx + 1 < len(aitems):
                                mla_qk(idx + 1)
                            mla_rest(idx)
                        if pair == 1:
                            K.barrier()
                            espp.close()
                            esp.close()
            if dbg and l == dbg.get("_layer", 0):
                dump_bf("ocT", ocT[:].rearrange("p a s -> p (a s)"), [128, 2 * S], [], es)
                K.barrier()

        def mixchunk(k, tsl):
            if k < 4:
                return qT[:, k, tsl]
            if k < 6:
                return uT[:, k - 4, tsl]
            return ocT[:, k - 6, tsl]

        sqj = esL2.enter_context(ST("sqj", [128, D], BF16))
        pre_gu = {}
        g2p = gcol[:, 2 * l + 1, :].unsqueeze(2).broadcast_to([128, 8, 128])
        for f_ in range(2):
            sgp = 8 + (ring_alloc() % 4)
            sup = 8 + (ring_alloc() % 4)
            for (sl, wsrc) in ((sgp, wg_d), (sup, wu_d)):
                rv = ring[sl][:].rearrange("p (k c) -> p k c", k=8)
                wload(wsrc[l].rearrange("(k p) c -> p k c", p=128)[:, :, f_ * 128:(f_ + 1) * 128], sl, rv)
            for sl in (sgp, sup):
                rv = ring[sl][:].rearrange("p (k c) -> p k c", k=8)
                K.op(POOL, lambda rv=rv: POOL.e.tensor_tensor(out=rv, in0=rv, in1=g2p, op=ALU.mult), R=[("ring", sl), "gcol"], W=[("ring", sl)])
            pre_gu[f_] = (sgp, sup)
        for t in range(NT):
            tsl = slice(t * 128, (t + 1) * 128)
            for hf in range(2):
                b = nbank()
                K.mm(R=[("ring", s_) for s_ in wo_units], W=[("ps", b)], mms=[(ps[:, b, :], mixchunk(k, tsl), ring[wo_units[k]][:, hf * 512:(hf + 1) * 512], k == 0, k == 7) for k in range(8)])
                K.op(DVE, lambda t=t, hf=hf, b=b: DVE.e.tensor_tensor(out=x_sb[:, t, hf * 512:(hf + 1) * 512], in0=x_sb[:, t, hf * 512:(hf + 1) * 512], in1=ps[:, b, :], op=ALU.add), R=[("ps", b), ("x", t)], W=[("x", t)])
            K.op(ACT, lambda t=t: ACT.e.activation(out=sqj[:], in_=x_sb[:, t, :], func=AF.Square, accum_out=ss[:, t:t + 1]), R=[("x", t)], W=["sqj", ("ss", t)])
        rstd_from_ss(ss[:, 0:NT], rstd[:, 0:NT], NT, D, [("ss", t) for t in range(NT)], [("rstd", t) for t in range(NT)], None)
        K.barrier(keep=lambda k: isinstance(k, tuple) and k[0] == "rstd")
        esL2.close()
        esL.close()
        if dbg and l == dbg.get("_layer", 0):
            dbg_dump("x_mid", x_sb[:].rearrange("p t d -> p (t d)"), [])
            K.barrier()

        with ExitStack() as es:
            T_ = lambda nm, shp, dtp=F32: es.enter_context(ST(nm, shp, dtp))
            h2T = T_("h2T", [128, 8, S], BF16)
            actT = T_("actT", [128, 8, S], BF16)
            junk = T_("junk2", [128, D], BF16)
            hb = [T_("hb2_%d" % i, [128, D], BF16) for i in range(2)]
            sl_ = [T_("silu%d" % i, [128, 512], BF16) for i in range(2)]
            def norm2_chunk(c):
                norm_tiles(list(range(4 * c, 4 * c + 4)), lambda i, c=c: h2T[:, :, (4 * c + i) * 128:(4 * c + i + 1) * 128], lambda i, c=c: ("h2T", c, i), junk, hb, stats=False)

            norm2_chunk(0)
            g2 = gcol[:, 2 * l + 1, :].unsqueeze(2).broadcast_to([128, 8, 128])
            wgv = wg_d[l].rearrange("(k p) c -> p k c", p=128)
            wuv = wu_d[l].rearrange("(k p) c -> p k c", p=128)
            thirds = [(0, 8), (8, 15), (15, 22)]
            si = 0
            for (f0, f1) in thirds:
                wd_units = [f - f0 for f in range(f0, f1)]
                for f in range(f0, f1):
                    if f in pre_gu:
                        sg_, su_ = pre_gu[f]
                    else:
                        sg_ = 8 + (ring_alloc() % 4)
                        su_ = 8 + (ring_alloc() % 4)
                        for (sl, wv_) in ((sg_, wgv), (su_, wuv)):
                            rv = ring[sl][:].rearrange("p (k c) -> p k c", k=8)
                            wload(wv_[:, :, f * 128:(f + 1) * 128], sl, rv)
                        for (sl, wv_) in ((sg_, wgv), (su_, wuv)):
                            rv = ring[sl][:].rearrange("p (k c) -> p k c", k=8)
                            K.op(POOL, lambda rv=rv: POOL.e.tensor_tensor(out=rv, in0=rv, in1=g2, op=ALU.mult), R=[("ring", sl), "gcol"], W=[("ring", sl)])
                    if f == f0 + 1:
                        for f_ in range(f0, f1):
                            wload(wd_d[l, f_ * 128:(f_ + 1) * 128, :], f_ - f0)
                    rg = ring[sg_][:].rearrange("p (k c) -> p k c", k=8)
                    ru = ring[su_][:].rearrange("p (k c) -> p k c", k=8)
                    for c in range(4):
                        tok = slice(c * 512, (c + 1) * 512)
                        if f == 0 and c + 1 < 4:
                            norm2_chunk(c + 1)
                        bg = nbank()
                        K.mm(R=[("ring", sg_)] + [("h2T", c, i_) for i_ in range(4)], W=[("ps", bg)], mms=[(ps[:, bg, :], rg[:, k, :], h2T[:, k, tok], k == 0, k == 7) for k in range(8)])
                        bu = nbank()
                        K.mm(R=[("ring", su_)] + [("h2T", c, i_) for i_ in range(4)], W=[("ps", bu)], mms=[(ps[:, bu, :], ru[:, k, :], h2T[:, k, tok], k == 0, k == 7) for k in range(8)])
                        sb_ = sl_[si % 2]
                        K.op(ACT, lambda sb_=sb_, bg=bg: ACT.e.activation(out=sb_[:], in_=ps[:, bg, :], func=AF.Silu), R=[("ps", bg)], W=[("silu", si % 2)])
                        K.op(DVE, lambda sb_=sb_, bu=bu, f=f, f0=f0, tok=tok: DVE.e.tensor_tensor(out=actT[:, f - f0, tok], in0=sb_[:], in1=ps[:, bu, :], op=ALU.mult), R=[("silu", si % 2), ("ps", bu)], W=[("actT", f - f0, c)])
                        si += 1
                nf = f1 - f0
                for t in range(NT):
                    tsl = slice(t * 128, (t + 1) * 128)
                    for hf in range(2):
                        b = nbank()
                        K.mm(R=[("ring", s_) for s_ in wd_units] + [("actT", i, t // 4) for i in range(nf)], W=[("ps", b)],
                             mms=[(ps[:, b, :], actT[:, i, tsl], ring[wd_units[i]][:, hf * 512:(hf + 1) * 512], i == 0, i == nf - 1) for i in range(nf)])
                        K.op(DVE, lambda t=t, hf=hf, b=b: DVE.e.tensor_tensor(out=x_sb[:, t, hf * 512:(hf + 1) * 512], in0=x_sb[:, t, hf * 512:(hf + 1) * 512], in1=ps[:, b, :], op=ALU.add), R=[("ps", b), ("x", t)], W=[("x", t)])
                        if hf == 1 and l == nlayers - 1 and f1 == NF:
                            K.op(ACT, lambda t=t: ACT.e.activation(out=junk[:], in_=x_sb[:, t, :], func=AF.Square, accum_out=ss[:, t:t + 1]), R=[("x", t)], W=["junk", ("ss", t)])
            if l == nlayers - 1:
                rstd_from_ss(ss[:, 0:NT], rstd[:, 0:NT], NT, D, [("ss", t) for t in range(NT)], [("rstd", t) for t in range(NT)], None)
            snapF = K.snapshot()
            if l + 1 < nlayers:
                load_w_in(l + 1)
            K.barrier(snapF, keep=isring)
        if dbg and l == dbg.get("_layer", 0):
            dbg_dump("x_out", x_sb[:].rearrange("p t d -> p (t d)"), [])
            K.barrier()

    with ExitStack() as es:
        fg = es.enter_context(ST("fg", [128, D], F32))
        junk = es.enter_context(ST("junk3", [128, D], BF16))
        ob = [es.enter_context(ST("ob%d" % i, [128, D], F32)) for i in range(2)]
        fsem = K.dsem("fg")
        osem = K.dsem("out")
        K.dma(SP, fg[:], fg_d.partition_broadcast(128), R=(), W=["fg"], sem=fsem)
        orr = out_d.rearrange("(t p) d -> p t d", p=128)
        for t in range(NT):
            o_ = ob[t % 2]
            K.op(DVE, lambda t=t, o_=o_: DVE.e.scalar_tensor_tensor(out=o_[:], in0=x_sb[:, t, :], scalar=rstd[:, t:t + 1], in1=fg[:], op0=ALU.mult, op1=ALU.mult), R=[("x", t), ("rstd", t), "fg"], W=[("ob", t % 2)])
            K.dma(SP, orr[:, t, :], o_[:], R=[("ob", t % 2)], W=(), sem=osem)
        K.barrier()
    es0.close()
    return nc


_CACHE = {}


def kernel(**inputs):
    if "nc" not in _CACHE:
        _CACHE["nc"] = build()
    nc = _CACHE["nc"]
    B = inputs["x"].shape[0]
    shared = {k: np.ascontiguousarray(np.asarray(v)) for k, v in inputs.items() if k not in ("x", "positions")}
    x = np.asarray(inputs["x"])
    pos = np.asarray(inputs["positions"]).astype(np.int32)
    in_maps = []
    for b in range(B):
        m = dict(shared)
        m["x"] = np.ascontiguousarray(x[b])
        m["positions"] = np.ascontiguousarray(pos[b])
        in_maps.append(m)
    res = run_bass_kernel_spmd(nc, in_maps, core_ids=list(range(B)))
    return np.stack([np.asarray(r["out"]) for r in res.results], axis=0).astype(np.float32)
```
